# Optimizing a Trainium2 kernel written in Bass

```python
import jax, jax.numpy as jnp
from jax import lax
import numpy as np

D_MODEL = 1024
BATCH = 8
SEQ = 2048
DEPTH = 4

CHUNK = 64
N_MEM = 256
N_A_LAYERS = DEPTH // 2
N_B_LAYERS = DEPTH - N_A_LAYERS
MAIN_WIDTH = 3 * D_MODEL // 4
MEM_WIDTH = D_MODEL // 4
MIX_WIDTH = MAIN_WIDTH + MEM_WIDTH
HG_HEAD_DIM = 128
HG_HEADS = MAIN_WIDTH // HG_HEAD_DIM
FOX_HEAD_DIM = 64
FOX_HEADS = MAIN_WIDTH // FOX_HEAD_DIM
MEM_HEADS = 4
MEM_HEAD_DIM = MEM_WIDTH // MEM_HEADS
D_FF = 2816
Q_BLOCK = 128
EPS = 1e-6
A_IN_WIDTH = 4 * MAIN_WIDTH + MEM_WIDTH
B_IN_WIDTH = 2 * MAIN_WIDTH + MEM_WIDTH
KV_WIDTH = 2 * MAIN_WIDTH + FOX_HEADS

kernel_name = 'hybrid_hgrn2_fox_yoco_macaron'


def rms_norm(x, gain):
    x32 = x.astype(jnp.float32)
    y = x32 * lax.rsqrt(jnp.mean(x32 * x32, axis=-1, keepdims=True) + EPS)
    return (y * gain.astype(jnp.float32)).astype(x.dtype)


def swiglu(h, w_gate, w_up, w_down):
    return (jax.nn.silu(h @ w_gate) * (h @ w_up)) @ w_down


def split_heads(t, n_heads):
    b, s, _ = t.shape
    return t.reshape(b, s, n_heads, -1).transpose(0, 2, 1, 3)


def merge_heads(t):
    b, h, s, d = t.shape
    return t.transpose(0, 2, 1, 3).reshape(b, s, h * d)


def hgrn2_recurrence(q, k, v, log_f):
    b, h, s, dk = q.shape
    dv = v.shape[-1]
    n = s // CHUNK

    def to_chunks(t):
        return t.reshape(b, h, n, CHUNK, t.shape[-1]).transpose(2, 0, 1, 3, 4)

    qc, kc, vc = to_chunks(q), to_chunks(k), to_chunks(v)
    cum = jnp.cumsum(to_chunks(log_f), axis=-2)
    tri = jnp.tril(jnp.ones((CHUNK, CHUNK), dtype=bool))

    def step(state, inp):
        q_, k_, v_, c_ = inp
        diff = c_[:, :, :, None, :] - c_[:, :, None, :, :]
        decay = jnp.exp(jnp.where(tri[:, :, None], diff, -jnp.inf))
        scores = jnp.einsum('bhtd,bhsd,bhtsd->bhts', q_, k_, decay)
        o = (jnp.einsum('bhts,bhsv->bhtv', scores, v_)
             + jnp.einsum('bhtd,bhdv->bhtv', q_ * jnp.exp(c_), state))
        c_end = c_[:, :, -1, :]
        state = (jnp.exp(c_end)[..., None] * state
                 + jnp.einsum('bhsd,bhsv->bhdv', k_ * jnp.exp(c_end[:, :, None, :] - c_), v_))
        return state, o

    state0 = jnp.zeros((b, h, dk, dv), jnp.float32)
    _, o = lax.scan(step, state0, (qc, kc, vc, cum))
    return o.transpose(1, 2, 0, 3, 4).reshape(b, h, s, dv)


def forgetting_attention(q, k, v, cum_log_f):
    s = q.shape[2]
    scale = FOX_HEAD_DIM ** -0.5
    outs = []
    for blk in range(s // Q_BLOCK):
        start = blk * Q_BLOCK
        end = start + Q_BLOCK
        logits = jnp.einsum('bhqd,bhkd->bhqk', q[:, :, start:end], k[:, :, :end]).astype(jnp.float32) * scale
        logits = logits + cum_log_f[:, :, start:end, None] - cum_log_f[:, :, None, :end]
        causal = jnp.arange(end)[None, :] <= jnp.arange(start, end)[:, None]
        p = jax.nn.softmax(jnp.where(causal, logits, -jnp.inf), axis=-1)
        outs.append(jnp.einsum('bhqk,bhkd->bhqd', p.astype(v.dtype), v[:, :, :end]))
    return jnp.concatenate(outs, axis=2)


def memory_attention(qm_raw, mem_n, w_mem_kv, q_gain, k_gain):
    kv = mem_n @ w_mem_kv
    km = rms_norm(split_heads(kv[..., :MEM_WIDTH], MEM_HEADS), k_gain)
    vm = split_heads(kv[..., MEM_WIDTH:], MEM_HEADS)
    qm = rms_norm(split_heads(qm_raw, MEM_HEADS), q_gain)
    logits = jnp.einsum('bhqd,bhkd->bhqk', qm, km).astype(jnp.float32) * (MEM_HEAD_DIM ** -0.5)
    p = jax.nn.softmax(logits, axis=-1)
    return merge_heads(jnp.einsum('bhqk,bhkd->bhqd', p.astype(vm.dtype), vm))


def setup_inputs(seed: int = 0) -> dict:
    key = jax.random.key(seed)
    ks = jax.random.split(key, 32)

    def w(k, shape, fan_in):
        return jax.random.normal(k, shape, jnp.float32) * (fan_in ** -0.5)

    def g(k, shape):
        return 1.0 + 0.02 * jax.random.normal(k, shape, jnp.float32)

    return {
        'x': jax.random.normal(ks[0], (BATCH, SEQ, D_MODEL), jnp.float32),
        'mem': jax.random.normal(ks[1], (BATCH, N_MEM, D_MODEL), jnp.float32),
        'ffn1_norm': g(ks[2], (DEPTH, D_MODEL)),
        'ffn1_w_gate': w(ks[3], (DEPTH, D_MODEL, D_FF), D_MODEL),
        'ffn1_w_up': w(ks[4], (DEPTH, D_MODEL, D_FF), D_MODEL),
        'ffn1_w_down': w(ks[5], (DEPTH, D_FF, D_MODEL), D_FF),
        'mix_norm': g(ks[6], (DEPTH, D_MODEL)),
        'mem_norm': g(ks[7], (DEPTH, D_MODEL)),
        'w_mem_kv': w(ks[8], (DEPTH, D_MODEL, 2 * MEM_WIDTH), D_MODEL),
        'mem_q_gain': g(ks[9], (DEPTH, MEM_HEAD_DIM)),
        'mem_k_gain': g(ks[10], (DEPTH, MEM_HEAD_DIM)),
        'w_in_a': w(ks[11], (N_A_LAYERS, D_MODEL, A_IN_WIDTH), D_MODEL),
        'hgrn_lb_logits': jax.random.normal(ks[12], (N_A_LAYERS, MAIN_WIDTH), jnp.float32),
        'hgrn_o_gain': g(ks[13], (N_A_LAYERS, HG_HEAD_DIM)),
        'w_in_b': w(ks[14], (N_B_LAYERS, D_MODEL, B_IN_WIDTH), D_MODEL),
        'fox_q_gain': g(ks[15], (N_B_LAYERS, FOX_HEAD_DIM)),
        'kv_norm': g(ks[16], (D_MODEL,)),
        'w_kv': w(ks[17], (D_MODEL, KV_WIDTH), D_MODEL),
        'fox_f_bias': 0.1 * jax.random.normal(ks[18], (FOX_HEADS,), jnp.float32),
        'fox_k_gain': g(ks[19], (FOX_HEAD_DIM,)),
        'w_out': w(ks[20], (DEPTH, MIX_WIDTH, D_MODEL), MIX_WIDTH),
        'ffn2_norm': g(ks[21], (DEPTH, D_MODEL)),
        'ffn2_w_gate': w(ks[22], (DEPTH, D_MODEL, D_FF), D_MODEL),
        'ffn2_w_up': w(ks[23], (DEPTH, D_MODEL, D_FF), D_MODEL),
        'ffn2_w_down': w(ks[24], (DEPTH, D_FF, D_MODEL), D_FF),
    }


def reference(x, mem, ffn1_norm, ffn1_w_gate, ffn1_w_up, ffn1_w_down, mix_norm, mem_norm,
              w_mem_kv, mem_q_gain, mem_k_gain, w_in_a, hgrn_lb_logits, hgrn_o_gain,
              w_in_b, fox_q_gain, kv_norm, w_kv, fox_f_bias, fox_k_gain, w_out,
              ffn2_norm, ffn2_w_gate, ffn2_w_up, ffn2_w_down):
    lb = jnp.cumsum(jax.nn.softmax(hgrn_lb_logits.astype(jnp.float32), axis=0), axis=0)
    lb = lb - lb[0:1]
    k_sh = v_sh = cum_log_f = None
    for l in range(DEPTH):
        x = x + 0.5 * swiglu(rms_norm(x, ffn1_norm[l]), ffn1_w_gate[l], ffn1_w_up[l], ffn1_w_down[l])
        h = rms_norm(x, mix_norm[l])
        mem_n = rms_norm(mem, mem_norm[l])
        if l < N_A_LAYERS:
            proj = h @ w_in_a[l]
            q_raw = proj[..., :MAIN_WIDTH]
            f_raw = proj[..., MAIN_WIDTH:2 * MAIN_WIDTH]
            i_raw = proj[..., 2 * MAIN_WIDTH:3 * MAIN_WIDTH]
            g_raw = proj[..., 3 * MAIN_WIDTH:4 * MAIN_WIDTH]
            qm_raw = proj[..., 4 * MAIN_WIDTH:]
            f = lb[l] + (1.0 - lb[l]) * jax.nn.sigmoid(f_raw.astype(jnp.float32))
            q = jax.nn.silu(q_raw.astype(jnp.float32))
            o = hgrn2_recurrence(split_heads(q, HG_HEADS), split_heads(1.0 - f, HG_HEADS),
                                 split_heads(i_raw.astype(jnp.float32), HG_HEADS),
                                 split_heads(jnp.log(f), HG_HEADS))
            main = merge_heads(rms_norm(o, hgrn_o_gain[l])) * jax.nn.silu(g_raw.astype(jnp.float32))
        else:
            j = l - N_A_LAYERS
            proj = h @ w_in_b[j]
            q = rms_norm(split_heads(proj[..., :MAIN_WIDTH], FOX_HEADS), fox_q_gain[j])
            gate = proj[..., MAIN_WIDTH:2 * MAIN_WIDTH]
            qm_raw = proj[..., 2 * MAIN_WIDTH:]
            o = forgetting_attention(q, k_sh, v_sh, cum_log_f)
            main = merge_heads(o) * jax.nn.sigmoid(gate)
        mem_o = memory_attention(qm_raw, mem_n, w_mem_kv[l], mem_q_gain[l], mem_k_gain[l])
        mixed = jnp.concatenate([main.astype(x.dtype), mem_o.astype(x.dtype)], axis=-1)
        x = x + mixed @ w_out[l]
        x = x + 0.5 * swiglu(rms_norm(x, ffn2_norm[l]), ffn2_w_gate[l], ffn2_w_up[l], ffn2_w_down[l])
        if l == N_A_LAYERS - 1:
            kvf = rms_norm(x, kv_norm) @ w_kv
            k_sh = rms_norm(split_heads(kvf[..., :MAIN_WIDTH], FOX_HEADS), fox_k_gain)
            v_sh = split_heads(kvf[..., MAIN_WIDTH:2 * MAIN_WIDTH], FOX_HEADS)
            log_f = jax.nn.log_sigmoid(kvf[..., 2 * MAIN_WIDTH:].astype(jnp.float32) + fox_f_bias.astype(jnp.float32))
            cum_log_f = jnp.cumsum(log_f.transpose(0, 2, 1), axis=-1)
    return x
```

```python
import contextlib
import numpy as np
import concourse.bass as bass
import concourse.mybir as mybir
from concourse.bass_utils import run_bass_kernel_spmd

F32 = mybir.dt.float32
BF16 = mybir.dt.bfloat16
AF = mybir.ActivationFunctionType
ALU = mybir.AluOpType


class Reg:
    __slots__ = ("w", "r", "excl")

    def __init__(self):
        self.w = []
        self.r = []
        self.excl = False


class Tile:
    def __init__(self, t):
        self.t = t
        self.regs = {}

    def r(self, key=None):
        g = self.regs.get(key)
        if g is None:
            g = self.regs[key] = Reg()
        return g

    def __getitem__(self, idx):
        return self.t[idx]


class Op:
    __slots__ = ("eng", "fn", "deps", "signal", "sigval", "dma", "dsem", "dval", "dprev")


class Sched:
    ENGS = ("pe", "act", "dve", "pool", "sp")

    def __init__(self, nc, ndma=12):
        self.nc = nc
        self.ops = []
        self.byeng = {e: [] for e in self.ENGS}
        self.ndma = ndma
        self.dma_count = {e: 0 for e in self.ENGS}
        self.dma_semcount = {}

    def _record(self, eng, fn, reads, writes, dma):
        i = len(self.ops)
        o = Op()
        o.eng = eng
        o.fn = fn
        o.signal = False
        o.sigval = 0
        o.dma = dma
        o.dsem = None
        o.dval = 0
        o.dprev = 0
        deps = set()
        for r in reads:
            deps.update(r.w)
            if r.excl:
                deps.update(d for d in r.r if self.ops[d].eng != eng)
        for w in writes:
            deps.update(w.w)
            deps.update(w.r)
        for r in reads:
            r.r.append(i)
        for w in writes:
            w.w = [i]
            w.r = []
        red = {}
        dd = []
        for d in deps:
            do = self.ops[d]
            if do.dma:
                dd.append(d)
            else:
                if do.eng == "pe" and eng == "pe" and not dma:
                    continue
                if red.get(do.eng, -1) < d:
                    red[do.eng] = d
        o.deps = dd + list(red.values())
        for d in o.deps:
            self.ops[d].signal = True
        if dma:
            n = self.dma_count[eng]
            self.dma_count[eng] = n + 1
            key = (eng, n % self.ndma)
            c = self.dma_semcount.get(key, 0)
            o.dsem = key
            o.dprev = 16 * c
            o.dval = 16 * (c + 1)
            self.dma_semcount[key] = c + 1
        self.ops.append(o)
        self.byeng[eng].append(i)
        return i

    def op(self, eng, fn, reads=(), writes=()):
        return self._record(eng, fn, reads, writes, False)

    def dma(self, eng, fn, reads=(), writes=()):
        return self._record(eng, fn, reads, writes, True)

    def emit(self, stack):
        nc = self.nc
        esem = {e: stack.enter_context(nc.semaphore("s_" + e)) for e in self.ENGS}
        dsem = {}
        for key in self.dma_semcount:
            dsem[key] = stack.enter_context(nc.semaphore("d_%s%d" % key))
        for e in self.ENGS:
            c = 0
            for i in self.byeng[e]:
                o = self.ops[i]
                if o.signal and not o.dma:
                    c += 1
                    o.sigval = c
        block = stack.enter_context(nc.Block())
        ops = self.ops

        def run(e, h):
            wm = {}
            for i in self.byeng[e]:
                o = ops[i]
                waits = {}
                for d in o.deps:
                    do = ops[d]
                    if do.dma:
                        s, v = dsem[do.dsem], do.dval
                    else:
                        s, v = esem[do.eng], do.sigval
                    k = id(s)
                    if waits.get(k, (0,))[0] < v:
                        waits[k] = (v, s)
                if o.dma and o.dprev > 0:
                    s = dsem[o.dsem]
                    k = id(s)
                    if waits.get(k, (0,))[0] < o.dprev:
                        waits[k] = (o.dprev, s)
                for k, (v, s) in waits.items():
                    if wm.get(k, 0) >= v:
                        continue
                    h.wait_ge(s, v)
                    wm[k] = v
                if o.fn is None:
                    continue
                ins = o.fn(h)
                if o.dma:
                    ins.then_inc(dsem[o.dsem], 16)
                elif o.signal:
                    ins.then_inc(esem[e], 1)

        @block.tensor
        def _(h):
            run("pe", h)

        @block.scalar
        def _(h):
            run("act", h)

        @block.vector
        def _(h):
            run("dve", h)

        @block.gpsimd
        def _(h):
            run("pool", h)

        @block.sync
        def _(h):
            run("sp", h)


S, D, NL, TG, NG = 2048, 1024, 4, 512, 4
DFF = 2816
EPS = 1e-6
NEG = -30000.0

V_FFN1, V_MIX, V_FFN2, V_MEMN, V_KVN = 0, 32, 64, 96, 128
V_OG, V_LB, V_FQG, V_FKG, V_MQG, V_MKG, V_FB = 136, 138, 150, 152, 153, 157, 161
NV = 162
CF_ID, CF_CNEG, CF_CMASK = 0, 128, 256
NCF = 768
CB_ID, CB_TRI, CB_BD, CB_ONES, CB_SEL = 0, 128, 256, 384, 512
NCB = 512 + 12 * 128


def host_consts():
    import ml_dtypes
    cf = np.zeros((128, NCF), np.float32)
    cf[:, CF_ID:CF_ID + 128] = np.eye(128, dtype=np.float32)
    s = np.arange(128)[:, None]
    t = np.arange(128)[None, :]
    cf[:, CF_CNEG:CF_CNEG + 128] = np.where(t >= s, 0.0, NEG)
    cm = np.ones((128, 512), np.float32)
    cm[:, 0::64] = 0.0
    cf[:, CF_CMASK:CF_CMASK + 512] = cm
    cb = np.zeros((128, NCB), np.float32)
    cb[:, CB_ID:CB_ID + 128] = np.eye(128)
    cb[:, CB_TRI:CB_TRI + 128] = ((t >= s) & ((t // 64) == (s // 64)))
    cb[:, CB_BD:CB_BD + 128] = ((t // 64) == (s // 64))
    cb[:, CB_ONES:CB_ONES + 128] = 1.0
    for r in range(3):
        for grp in range(4):
            for i in range(3):
                h = 3 * grp + i
                cb[h, CB_SEL + (r * 4 + grp) * 128 + 32 * i + r] = 1.0
    return cf, cb.astype(ml_dtypes.bfloat16)


def host_vecs(inp):
    v = np.zeros((128, NV), np.float32)

    def fm(a):
        return np.ascontiguousarray(a.reshape(-1, 128).T)

    for l in range(NL):
        v[:, V_FFN1 + 8 * l:V_FFN1 + 8 * l + 8] = fm(inp["ffn1_norm"][l])
        v[:, V_MIX + 8 * l:V_MIX + 8 * l + 8] = fm(inp["mix_norm"][l])
        v[:, V_FFN2 + 8 * l:V_FFN2 + 8 * l + 8] = fm(inp["ffn2_norm"][l])
        v[:, V_MEMN + 8 * l:V_MEMN + 8 * l + 8] = fm(inp["mem_norm"][l])
        v[:, V_MQG + l] = np.tile(inp["mem_q_gain"][l], 2)
        v[:, V_MKG + l] = np.tile(inp["mem_k_gain"][l], 2)
    v[:, V_KVN:V_KVN + 8] = fm(inp["kv_norm"])
    for l in range(2):
        v[:, V_OG + l] = inp["hgrn_o_gain"][l]
        v[:, V_LB + 6 * l:V_LB + 6 * l + 6] = fm(inp["hgrn_lb_logits"][l])
        v[:, V_FQG + l] = np.tile(inp["fox_q_gain"][l], 2)
    v[:, V_FKG] = np.tile(inp["fox_k_gain"], 2)
    v[0:12, V_FB] = inp["fox_f_bias"]
    return v


def build(n_groups=NG, layers=(0, 1, 2, 3), want_kv=True, phases=None, shrink=False):
    NLW = 1 if shrink else NL
    PH = (lambda p: True) if phases is None else (lambda p: p in phases)
    nc = bass.Bass("TRN2", target_bir_lowering=False)

    def din(name, shape, dt=F32):
        return nc.dram_tensor(name, list(shape), dt, kind="ExternalInput").ap()

    x_d = din("x", [S, D])
    mem_d = din("mem", [256, D])
    vecs_d = din("vecs", [128, NV])
    cstf_d = din("cstf", [128, NCF])
    cstb_d = din("cstb", [128, NCB], BF16)
    wg_d = [din("ffn1_w_gate", [NLW, D, DFF]), din("ffn2_w_gate", [NLW, D, DFF])]
    wu_d = [din("ffn1_w_up", [NLW, D, DFF]), din("ffn2_w_up", [NLW, D, DFF])]
    wd_d = [din("ffn1_w_down", [NLW, DFF, D]), din("ffn2_w_down", [NLW, DFF, D])]
    wmem_d = din("w_mem_kv", [NLW, D, 512])
    wina_d = din("wina", [2, D, 3328])
    winb_d = din("winb", [2, D, 1792])
    wkv_d = din("w_kv", [D, 1548])
    wo_d = din("w_out", [NLW, D, D])
    out_d = nc.dram_tensor("out", [S, D], F32, kind="ExternalOutput").ap()

    with contextlib.ExitStack() as st:
        k = Sched(nc)

        def sb(name, shape, dt):
            return Tile(st.enter_context(nc.sbuf_tensor("s_" + name, list(shape), dt)))

        banks = [Tile(st.enter_context(nc.psum_tensor("pb%d" % i, [128, 512], F32))) for i in range(8)]
        for b_ in banks:
            b_.r().excl = True
        rotc = [0]
        rotn = [6]

        def rot():
            b = banks[rotc[0] % rotn[0]]
            rotc[0] += 1
            return b

        NUM, DEN = banks[6], banks[7]

        xg = sb("xg", [128, 8, TG], F32)
        h = sb("h", [128, 8, TG], BF16)
        act = sb("act", [128, 11, TG], BF16)
        mixed = sb("mixed", [128, 8, TG], BF16)
        kT = sb("kT", [128, 6, S], BF16)
        vv = sb("vv", [128, 16, 768], BF16)
        Dh = sb("Dh", [128, 4, S], BF16)
        negD = sb("negD", [128, 16, 12], F32)
        memh = sb("memh", [128, 8, 256], BF16)
        kmT = sb("kmT", [128, NL, 2, 256], BF16)
        vm = sb("vm", [128, NL, 2, 256], BF16)
        vecs = sb("vecs", [128, NV], F32)
        cstf = sb("cstf", [128, NCF], F32)
        cstb = sb("cstb", [128, NCB], BF16)
        der = sb("der", [128, 64], F32)
        fac = sb("fac", [128, 8, 8], F32)
        Sst = sb("Sst", [128, 2, 6, 128], F32)
        carry = sb("carry", [128, 1], F32)
        xin = sb("xin", [128, D], F32)
        xout = xin
        NSLOT = 4
        slots = [sb("w%d" % i, [128, 4096], BF16) for i in range(NSLOT)]
        NF, NB = 10, 10
        Fp = [sb("F%d" % i, [128, 512], F32) for i in range(NF)]
        Bp = [sb("B%d" % i, [128, 512], BF16) for i in range(NB)]
        Sb2 = [sb("Sb%d" % i, [128, 128], BF16) for i in range(8)]
        Tm2 = [sb("Tm%d" % i, [128, 128], F32) for i in range(3)]
        Xb = [sb("Xb%d" % i, [128, 512], BF16) for i in range(4)]
        fc_, bc_, sc_, tc_ = [0], [0], [0], [0]

        def Ft():
            t = Fp[8 + fc_[0] % 2]
            fc_[0] += 1
            return t

        def Bt():
            t = Xb[2 + bc_[0] % 2]
            bc_[0] += 1
            return t

        ptc_ = [0]

        def PTt():
            t = Bp[ptc_[0] % 4]
            ptc_[0] += 1
            return t

        def Tmt():
            t = Tm2[tc_[0] % 3]
            tc_[0] += 1
            return t

        ident = cstf[:, CF_ID:CF_ID + 128]
        cneg = cstf[:, CF_CNEG:CF_CNEG + 128]
        cmask = cstf[:, CF_CMASK:CF_CMASK + 512]
        identb = cstb[:, CB_ID:CB_ID + 128]
        trimask = cstb[:, CB_TRI:CB_TRI + 128]
        bd64 = cstb[:, CB_BD:CB_BD + 128]
        onesb = cstb[:, CB_ONES:CB_ONES + 128]
        CR = [cstf.r(), cstb.r(), vecs.r(), der.r()]

        def MM(out, lhsT, rhs, start, stop, R, W, skip=False):
            if skip:
                k.op("pe", lambda e: e.matmul(out, lhsT, rhs, start=start, stop=stop, skip_group_check=True), R, W)
            else:
                k.op("pe", lambda e: e.matmul(out, lhsT, rhs, start=start, stop=stop), R, W)

        def TR(out, in_, idn, R, W):
            k.op("pe", lambda e: e.transpose(out, in_, idn), R, W)

        def ACT(out, in_, func, R, W, bias=None, scale=None):
            kw = {}
            if bias is not None:
                kw["bias"] = bias
            if scale is not None:
                kw["scale"] = scale
            k.op("act", lambda e: e.activation(out, in_, func, **kw), R, W)

        def TT(out, in0, in1, op, R, W, eng="dve"):
            k.op(eng, lambda e: e.tensor_tensor(out, in0, in1, op), R, W)

        def TS(out, in0, s1, s2, op0, op1, R, W, eng="dve"):
            if s2 is None:
                k.op(eng, lambda e: e.tensor_scalar(out, in0, s1, None, op0), R, W)
            else:
                k.op(eng, lambda e: e.tensor_scalar(out, in0, s1, s2, op0, op1), R, W)

        def STT(out, in0, sc, in1, op0, op1, R, W):
            k.op("dve", lambda e: e.scalar_tensor_tensor(out, in0, sc, in1, op0, op1), R, W)

        def CP(out, in_, R, W, eng="dve"):
            if eng == "act":
                k.op("act", lambda e: e.activation(out, in_, AF.Copy), R, W)
            else:
                k.op(eng, lambda e: e.tensor_copy(out, in_), R, W)

        def RECIP(out, in_, R, W):
            k.op("dve", lambda e: e.reciprocal(out, in_), R, W)

        def DMA(eng, out, in_, R, W):
            k.dma(eng, lambda e: e.dma_start(out=out, in_=in_), R, W)

        slotc = [0]

        def loadw(src, shape):
            t = slots[slotc[0] % NSLOT]
            slotc[0] += 1
            a, b = shape
            view = t[:, 0:a * b].rearrange("p (a b) -> p a b", a=a)
            DMA("pool", view, src, [], [t.r()])
            return view, t

        def wrows(w2d):
            return w2d.rearrange("(k p) n -> p k n", p=128)

        def rsqrt_from(ps_ap, scale, R_ps, W_t, out_ap):
            ACT(out_ap, ps_ap, AF.Ln, R_ps, W_t, bias=EPS, scale=scale)
            ACT(out_ap, out_ap, AF.Exp, W_t, W_t, scale=-0.5)

        DMA("sp", cstf[:, :], cstf_d, [], [cstf.r()])
        DMA("sp", cstb[:, :], cstb_d, [], [cstb.r()])
        DMA("sp", vecs[:, :], vecs_d, [], [vecs.r()])
        if PH("der"):
            k.op("dve", lambda e: e.memset(der[:, 0:6], 0.0), [], [der.r()])
            k.op("dve", lambda e: e.memset(der[:, 12:18], 1.0), [der.r()], [der.r()])
            TT(der[:, 6:12], vecs[:, V_LB + 6:V_LB + 12], vecs[:, V_LB:V_LB + 6], ALU.subtract, [vecs.r(), der.r()], [der.r()])
            ACT(der[:, 6:12], der[:, 6:12], AF.Sigmoid, [der.r()], [der.r()])
            TS(der[:, 18:24], der[:, 6:12], -1.0, 1.0, ALU.mult, ALU.add, [der.r()], [der.r()])
            TS(der[:, 24:26], vecs[:, V_FQG:V_FQG + 2], 0.125, None, ALU.mult, None, [vecs.r(), der.r()], [der.r()])
            TS(der[:, 26:30], vecs[:, V_MQG:V_MQG + 4], 0.125, None, ALU.mult, None, [vecs.r(), der.r()], [der.r()])
            TS(der[:, 30:31], vecs[:, V_FB:V_FB + 1], -1.0, None, ALU.mult, None, [vecs.r(), der.r()], [der.r()])
            TS(der[:, 32:44], der[:, 12:24], 0.5, None, ALU.mult, None, [der.r()], [der.r()])
            TT(der[:, 44:56], der[:, 32:44], der[:, 0:12], ALU.add, [der.r()], [der.r()])
            k.op("dve", lambda e: e.memset(Sst[:, :, :, :], 0.0), [], [Sst.r((l, hd)) for l in range(2) for hd in range(6)])
            k.op("dve", lambda e: e.memset(carry[:, :], 0.0), [], [carry.r()])

        def LB(l, hd):
            return der[:, 6 * l + hd:6 * l + hd + 1]

        def OML(l, hd):
            return der[:, 12 + 6 * l + hd:12 + 6 * l + hd + 1]

        def HOML(l, hd):
            return der[:, 32 + 6 * l + hd:32 + 6 * l + hd + 1]

        def HB(l, hd):
            return der[:, 44 + 6 * l + hd:44 + 6 * l + hd + 1]

        memT = [Fp[0], Fp[1], Fp[2], Fp[3]]

        def memT_ap(c):
            return memT[c // 2][:, (c % 2) * 256:(c % 2) * 256 + 256]

        for tb in (range(2) if PH("setup_mem") else ()):
            DMA("sp", xin[:, :], mem_d[tb * 128:(tb + 1) * 128, :], [], [xin.r()])
            for half in range(2):
                bk = rot()
                for c4 in range(4):
                    c = half * 4 + c4
                    TR(bk[:, c4 * 128:(c4 + 1) * 128], xin[:, c * 128:(c + 1) * 128], ident, [xin.r()] + CR, [bk.r()])
                for c4 in range(4):
                    c = half * 4 + c4
                    CP(memT_ap(c)[:, tb * 128:(tb + 1) * 128], bk[:, c4 * 128:(c4 + 1) * 128], [bk.r()], [memT[c // 2].r()])
        nsb = rot()
        for c in (range(8) if PH("setup_mem") else ()):
            sq = Bt()
            ACT(sq[:, 0:256], memT_ap(c), AF.Square, [memT[c // 2].r()], [sq.r()])
            MM(nsb[:, 0:256], onesb, sq[:, 0:256], c == 0, c == 7, [sq.r()] + CR, [nsb.r()])
        rsm = Fp[4]
        if PH("setup_mem"):
            rsqrt_from(nsb[:, 0:256], 1.0 / D, [nsb.r()], [rsm.r()], rsm[:, 0:256])
        for c in (range(8) if PH("setup_mem") else ()):
            TT(memh[:, c, :], memT_ap(c), rsm[:, 0:256], ALU.mult, [memT[c // 2].r(), rsm.r()], [memh.r()])
        memn = act
        for l in (range(NL) if PH("setup_mem") else ()):
            for c in range(8):
                TS(memn[:, c, 0:256], memh[:, c, :], vecs[:, V_MEMN + 8 * l + c:V_MEMN + 8 * l + c + 1], None, ALU.mult, None,
                   [memh.r()] + CR, [act.r()])
            wm, wmt = loadw(wrows(wmem_d[min(l, NLW - 1)]), (8, 512))
            for c2 in range(2):
                kp = rot()
                for c in range(8):
                    MM(kp[:, 0:256], wm[:, c, c2 * 128:(c2 + 1) * 128], memn[:, c, 0:256], c == 0, c == 7, [wmt.r(), act.r()], [kp.r()])
                sq = Bt()
                ACT(sq[:, 0:256], kp[:, 0:256], AF.Square, [kp.r()], [sq.r()])
                ns = rot()
                MM(ns[:, 0:256], bd64, sq[:, 0:256], True, True, [sq.r()] + CR, [ns.r()])
                rs = Ft()
                rsqrt_from(ns[:, 0:256], 1.0 / 64, [ns.r()], [rs.r()], rs[:, 0:256])
                STT(kmT[:, l, c2, :], kp[:, 0:256], vecs[:, V_MKG + l:V_MKG + l + 1], rs[:, 0:256], ALU.mult, ALU.mult,
                    [kp.r(), rs.r()] + CR, [kmT.r(l)])
            for tb in range(2):
                vp = rot()
                for c in range(8):
                    MM(vp[:, 0:256], memn[:, c, tb * 128:(tb + 1) * 128], wm[:, c, 256:512], c == 0, c == 7, [wmt.r(), act.r()], [vp.r()])
                CP(vm[:, l, tb, :], vp[:, 0:256], [vp.r()], [vm.r(l)])

        XR = [xg.r(c) for c in range(8)]

        def norm_x(gcol):
            ns = rot()
            for c in range(8):
                sq = Bt()
                ACT(sq[:, :], xg[:, c, :], AF.Square, [xg.r(c)], [sq.r()])
                MM(ns[:, :], onesb, sq[:, :], c == 0, c == 7, [sq.r()] + CR, [ns.r()])
            rs = Ft()
            rsqrt_from(ns[:, :], 1.0 / D, [ns.r()], [rs.r()], rs[:, :])
            for c in range(8):
                eng = "dve"
                k.op(eng, lambda e, c=c: e.scalar_tensor_tensor(h[:, c, :], xg[:, c, :], vecs[:, gcol + c:gcol + c + 1], rs[:, :],
                                                                ALU.mult, ALU.mult),
                     [xg.r(c), rs.r()] + CR, [h.r(c)])

        def ffn(l, which):
            norm_x((V_FFN1 if which == 0 else V_FFN2) + 8 * l)
            lw = min(l, NLW - 1)
            Wg, Wu, Wd = wrows(wg_d[which][lw]), wrows(wu_d[which][lw]), wd_d[which][lw].rearrange("(c p) d -> p c d", p=128)
            for half in range(2):
                f0 = half * 11
                fi = 0
                for nch in (4, 4, 3):
                    c0 = (f0 + fi) * 128
                    g_v, g_t = loadw(Wg[:, :, c0:c0 + nch * 128], (8, nch * 128))
                    u_v, u_t = loadw(Wu[:, :, c0:c0 + nch * 128], (8, nch * 128))
                    for j in range(nch):
                        G, U = rot(), rot()
                        for c in range(8):
                            MM(G[:, :], g_v[:, c, j * 128:(j + 1) * 128], h[:, c, :], c == 0, c == 7, [g_t.r(), h.r(c)], [G.r()])
                        for c in range(8):
                            MM(U[:, :], u_v[:, c, j * 128:(j + 1) * 128], h[:, c, :], c == 0, c == 7, [u_t.r(), h.r(c)], [U.r()])
                        sg = Ft()
                        ACT(sg[:, :], G[:, :], AF.Silu, [G.r()], [sg.r()])
                        TT(act[:, fi, :], sg[:, :], U[:, :], ALU.mult, [sg.r(), U.r()], [act.r()])
                        fi += 1
                for db in range(4):
                    d_v, d_t = loadw(Wd[:, f0:f0 + 11, db * 256:(db + 1) * 256], (11, 256))
                    for dc in range(2):
                        d = db * 2 + dc
                        Y = rot()
                        for f in range(11):
                            MM(Y[:, :], d_v[:, f, dc * 128:(dc + 1) * 128], act[:, f, :], f == 0, f == 10, [d_t.r(), act.r()], [Y.r()])
                        STT(xg[:, d, :], Y[:, :], 0.5, xg[:, d, :], ALU.mult, ALU.add, [Y.r(), xg.r(d)], [xg.r(d)])

        def attn_core(q, qreg, kT_ap, nkb, v_ap, vreg, kreg, out_ap, outreg, gate=None, gatereg=None,
                      aug=None, bias=None, causal_g=None, acc=None):
            NUM, DEN = acc if acc is not None else (banks[6], banks[7])
            def stageA(kb):
                res = []
                diag = causal_g is not None and kb >= 4 * causal_g
                c0 = (kb - 4 * causal_g) * 128 if diag else 0
                Ss = [rot(), rot()]
                for e in range(2):
                    rows = slice(64 * e, 64 * e + 64)
                    MM(Ss[e][:, c0:512], kT_ap(e)[:, kb * 128:(kb + 1) * 128], q[rows, c0:512], True, aug is None,
                       [kreg, qreg], [Ss[e].r()])
                if aug is not None:
                    for e in range(2):
                        o1, d1 = aug(e, c0)
                        MM(Ss[e][:, c0:512], o1, d1, False, True, [Dh.r()] + CR, [Ss[e].r()])
                for e in range(2):
                    Sb_ = Ss[e]
                    PT = PTt()
                    bi = bias(kb, e) if bias is not None else None
                    br = [negD.r()] if bias is not None else []
                    if diag:
                        tm = Tmt()
                        TT(tm[:, :], Sb_[:, c0:c0 + 128], cneg, ALU.add, [Sb_.r()] + CR, [tm.r()])
                        ACT(PT[:, c0:c0 + 128], tm[:, :], AF.Exp, [tm.r()] + br, [PT.r()], bias=bi)
                        if c0 + 128 < 512:
                            ACT(PT[:, c0 + 128:512], Sb_[:, c0 + 128:512], AF.Exp, [Sb_.r()] + br, [PT.r()], bias=bi)
                    else:
                        ACT(PT[:, c0:512], Sb_[:, c0:512], AF.Exp, [Sb_.r()] + br, [PT.r()], bias=bi)
                    res.append((PT, c0))
                return res

            def stageB(kb, res):
                for e in range(2):
                    PT, c0 = res[e]
                    rows = slice(64 * e, 64 * e + 64)
                    MM(NUM[rows, c0:512], v_ap(kb, e), PT[:, c0:512], kb == 0, kb == nkb - 1, [PT.r(), vreg], [NUM.r()])
                for e in range(2):
                    PT, c0 = res[e]
                    rows = slice(64 * e, 64 * e + 64)
                    MM(DEN[rows, c0:512], onesb[:, 0:64], PT[:, c0:512], kb == 0, kb == nkb - 1, [PT.r()] + CR, [DEN.r()])

            prev = stageA(0)
            for kb in range(nkb):
                nxt = stageA(kb + 1) if kb + 1 < nkb else None
                stageB(kb, prev)
                prev = nxt
            rec = Ft()
            if gate is not None:
                STT(rec[:, :], gate, 1.0, DEN[:, :], ALU.add, ALU.mult, [gatereg, DEN.r()], [rec.r()])
                RECIP(rec[:, :], rec[:, :], [rec.r()], [rec.r()])
            else:
                RECIP(rec[:, :], DEN[:, :], [DEN.r()], [rec.r()])
            TT(out_ap, NUM[:, :], rec[:, :], ALU.mult, [NUM.r(), rec.r()], [outreg])

        def headnorm_q(P, wq_scalar, width, dst=None):
            sq = Bt()
            ACT(sq[:, :], P[:, :], AF.Square, [P.r()], [sq.r()])
            ns = rot()
            MM(ns[:, :], bd64, sq[:, :], True, True, [sq.r()] + CR, [ns.r()])
            rs = Ft()
            rsqrt_from(ns[:, :], 1.0 / 64, [ns.r()], [rs.r()], rs[:, :])
            qn = dst if dst is not None else Bp[4]
            STT(qn[:, :], P[:, :], wq_scalar, rs[:, :], ALU.mult, ALU.mult, [P.r(), rs.r()] + CR, [qn.r()])
            return qn

        def mem_prep(l, dsts):
            if l < 2:
                src = wrows(wina_d[l])[:, :, 3072:3328]
            else:
                src = wrows(winb_d[l - 2])[:, :, 1536:1792]
            wq, wqt = loadw(src, (8, 256))
            qns = []
            for c2 in range(2):
                QM = rot()
                for c in range(8):
                    MM(QM[:, :], wq[:, c, c2 * 128:(c2 + 1) * 128], h[:, c, :], c == 0, c == 7, [wqt.r(), h.r(c)], [QM.r()])
                qns.append(headnorm_q(QM, der[:, 26 + l:27 + l], 64, dst=dsts[c2]))
            return qns

        class _View:
            def __init__(self, t):
                self.t = t
                self.ap = t[:, 0:256].bitcast(BF16)

            def r(self, key=None):
                return self.t.r(key)

            def __getitem__(self, idx):
                return self.ap[idx]

        def mem_attn(l, g, qns=None):
            rotn[0] = 4
            accs = ((banks[4], banks[5]), (banks[6], banks[7]))
            if qns is None:
                qns = mem_prep(l, (Bp[4], Bp[5]))
            for c2 in range(2):
                qn = qns[c2]
                attn_core(qn[:, :], qn.r(),
                          lambda e, c2=c2: kmT[64 * e:64 * e + 64, l, c2, :], 2,
                          lambda kb, e, c2=c2: vm[:, l, kb, (c2 * 2 + e) * 64:(c2 * 2 + e) * 64 + 64], vm.r(l), kmT.r(l),
                          mixed[:, 6 + c2, :], mixed.r(6 + c2), acc=accs[c2])
            rotn[0] = 6

        def w_out(l):
            Wo = wrows(wo_d[min(l, NLW - 1)])
            for hb in range(2):
                o_v, o_t = loadw(Wo[:, :, hb * 512:(hb + 1) * 512], (8, 512))
                for dd in range(4):
                    d = hb * 4 + dd
                    Y = rot()
                    for c in range(8):
                        MM(Y[:, :], o_v[:, c, dd * 128:(dd + 1) * 128], mixed[:, c, :], c == 0, c == 7, [o_t.r(), mixed.r(c)], [Y.r()])
                    TT(xg[:, d, :], Y[:, :], xg[:, d, :], ALU.add, [Y.r(), xg.r(d)], [xg.r(d)])

        def hgrn(l, g):
            Wa = wrows(wina_d[l])
            th, logf, cc, dq, eq, ek, qs, t1 = Fp[0:8]
            ktT, scT = Xb[0], Xb[1]

            B_Q, B_F, B_G, B_V, B_KA, B_KB, B_O, B_T = banks

            def S1p(hd):
                par = hd % 2
                wA, wAt = loadw(Wa[:, :, hd * 512:(hd + 1) * 512], (8, 512))
                Q, Fq, Gq, Vq = B_Q, B_F, B_G, B_V
                for c in range(8):
                    MM(Q[:, :], wA[:, c, 0:128], h[:, c, :], c == 0, c == 7, [wAt.r(), h.r(c)], [Q.r()])
                for c in range(8):
                    MM(Gq[:, :], wA[:, c, 384:512], h[:, c, :], c == 0, c == 7, [wAt.r(), h.r(c)], [Gq.r()])
                for c in range(8):
                    MM(Fq[:, :], wA[:, c, 128:256], h[:, c, :], c == 0, c == 7, [wAt.r(), h.r(c)], [Fq.r()])
                for tb in range(4):
                    for c in range(8):
                        MM(Vq[:, tb * 128:(tb + 1) * 128], h[:, c, tb * 128:(tb + 1) * 128], wA[:, c, 256:384], c == 0, c == 7,
                           [wAt.r(), h.r(c)], [Vq.r()])

            def S1f(hd):
                par = hd % 2
                qt, kt, gs, vtok, kt2 = Bp[5 * par:5 * par + 5]
                Q, Fq, Gq, Vq = B_Q, B_F, B_G, B_V
                ACT(qs[:, :], Q[:, :], AF.Silu, [Q.r()], [qs.r()])
                ACT(gs[:, :], Gq[:, :], AF.Silu, [Gq.r()], [gs.r()])
                ACT(th[:, :], Fq[:, :], AF.Tanh, [Fq.r()], [th.r()], scale=0.5)
                CP(vtok[:, :], Vq[:, :], [Vq.r()], [vtok.r()], eng="act")

            def S1e(hd):
                par = hd % 2
                qt, kt, gs, vtok, kt2 = Bp[5 * par:5 * par + 5]
                fo = 4 * par
                freg = fac.r(par)
                ACT(logf[:, :], th[:, :], AF.Ln, [th.r()] + CR, [logf.r()], bias=HB(l, hd), scale=HOML(l, hd))
                k.op("dve", lambda e: e.tensor_tensor_scan(cc[:, :], cmask, logf[:, :], 0.0, ALU.mult, ALU.add),
                     [logf.r()] + CR, [cc.r()])
                TT(fac[:, fo + 3, :], cc[:, 63:512:64], cc[:, 31:512:64], ALU.subtract, [cc.r()], [freg])
                cm_b = cc[:, 31:512:64].unsqueeze(2).to_broadcast([128, 8, 64])
                TT(dq[:, :].rearrange("p (a b) -> p a b", b=64), cc[:, :].rearrange("p (a b) -> p a b", b=64), cm_b, ALU.subtract,
                   [cc.r()], [dq.r()])
                ACT(fac[:, fo + 1, :], fac[:, fo + 3, :], AF.Exp, [freg], [freg])
                ACT(ek[:, :], dq[:, :], AF.Exp, [dq.r()], [ek.r()], scale=-1.0)
                ACT(eq[:, :], dq[:, :], AF.Exp, [dq.r()], [eq.r()])
                ACT(fac[:, fo + 0, :], cc[:, 63:512:64], AF.Exp, [cc.r(), freg], [freg])
                ACT(fac[:, fo + 2, :], cc[:, 31:512:64], AF.Exp, [cc.r(), freg], [freg])
                TS(t1[:, :], th[:, :], -1.0, 1.0, ALU.mult, ALU.add, [th.r()], [t1.r()])
                STT(kt[:, :], t1[:, :], HOML(l, hd), ek[:, :], ALU.mult, ALU.mult, [t1.r(), ek.r()] + CR, [kt.r()])
                TT(kt2[:, :].rearrange("p (a b) -> p a b", b=64), kt[:, :].rearrange("p (a b) -> p a b", b=64),
                   fac[:, fo + 1, :].unsqueeze(2).to_broadcast([128, 8, 64]), ALU.mult, [kt.r(), freg], [kt2.r()])
                TT(qt[:, :], qs[:, :], eq[:, :], ALU.mult, [qs.r(), eq.r()], [qt.r()])
                return qt, kt, kt2, gs, vtok, fo, freg

            def S2a(hd, ctx):
                qt, kt, kt2, gs, vtok, fo, freg = ctx
                tp = B_T
                tpb = tp[:, 0:256].bitcast(BF16)
                for tb in range(4):
                    TR(tpb[:, tb * 128:(tb + 1) * 128], kt2[:, tb * 128:(tb + 1) * 128], identb, [kt2.r()] + CR, [tp.r()])
                CP(ktT[:, :], tpb, [tp.r()], [ktT.r()], eng="act")
                KVb = [B_KA, B_KB]
                for n in range(8):
                    b = n // 2
                    rows = slice(64 * (n % 2), 64 * (n % 2) + 64)
                    MM(KVb[n % 2][:, b * 128:(b + 1) * 128], ktT[rows, b * 128:(b + 1) * 128], vtok[rows, b * 128:(b + 1) * 128],
                       True, True, [ktT.r(), vtok.r()], [KVb[n % 2].r()])
                SC = B_T
                for b in range(4):
                    MM(SC[:, b * 128:(b + 1) * 128], kt[:, b * 128:(b + 1) * 128], qt[:, b * 128:(b + 1) * 128], True, True,
                       [kt.r(), qt.r()], [SC.r()])
                TT(scT[:, :].rearrange("p (a b) -> p a b", b=128), SC[:, :].rearrange("p (a b) -> p a b", b=128),
                   trimask.unsqueeze(1).to_broadcast([128, 4, 128]), ALU.mult, [SC.r()] + CR, [scT.r()])
                O = B_O
                for b in range(4):
                    MM(O[:, b * 128:(b + 1) * 128], vtok[:, b * 128:(b + 1) * 128], scT[:, b * 128:(b + 1) * 128], b == 0, False,
                       [vtok.r(), scT.r()], [O.r()], skip=True)

            def S2c(hd, ctx):
                qt, kt, kt2, gs, vtok, fo, freg = ctx
                KVb = [B_KA, B_KB]
                O = B_O
                sreg = Sst.r((l, hd))
                Sap = Sst[:, l, hd, :]
                sbs = []
                for n in range(8):
                    sbt = Sb2[n]
                    TS(sbt[:, :], Sap, fac[:, fo + 2, n:n + 1], None, ALU.mult, None, [sreg, freg], [sbt.r()])
                    STT(Sap, Sap, fac[:, fo + 0, n:n + 1], KVb[n % 2][:, (n // 2) * 128:(n // 2 + 1) * 128], ALU.mult, ALU.add,
                        [sreg, freg, KVb[n % 2].r()], [sreg])
                    sbs.append(sbt)
                for n in range(8):
                    MM(O[:, n * 64:(n + 1) * 64], sbs[n][:, :], qt[:, n * 64:(n + 1) * 64], False, n == 7, [sbs[n].r(), qt.r()], [O.r()], skip=True)

            def S2t(hd, ctx):
                qt, kt, kt2, gs, vtok, fo, freg = ctx
                O = B_O
                sqo = Bt()
                ACT(sqo[:, :], O[:, :], AF.Square, [O.r()], [sqo.r()])
                ns = B_T
                MM(ns[:, :], onesb, sqo[:, :], True, True, [sqo.r()] + CR, [ns.r()])
                rs = Ft()
                rsqrt_from(ns[:, :], 1.0 / 128, [ns.r()], [rs.r()], rs[:, :])
                on = Ft()
                STT(on[:, :], O[:, :], vecs[:, V_OG + l:V_OG + l + 1], rs[:, :], ALU.mult, ALU.mult, [O.r(), rs.r()] + CR, [on.r()])
                TT(mixed[:, hd, :], on[:, :], gs[:, :], ALU.mult, [on.r(), gs.r()], [mixed.r(hd)])

            S1p(0)
            S1f(0)
            S1p(1)
            pending = S1e(0)
            for hd in range(6):
                cur = pending
                S2a(hd, cur)
                if hd + 1 < 6:
                    S1f(hd + 1)
                if hd + 2 < 6:
                    S1p(hd + 2)
                S2c(hd, cur)
                if hd + 1 < 6:
                    pending = S1e(hd + 1)
                S2t(hd, cur)

        def fox(l, g):
            j2 = l - 2
            Wb = wrows(winb_d[j2])
            rotn[0] = 4
            accs = ((banks[4], banks[5]), (banks[6], banks[7]))

            def prep(j):
                wB, wBt = loadw(Wb[:, :, j * 256:(j + 1) * 256], (8, 256))
                Qp, Gp = rot(), rot()
                for c in range(8):
                    MM(Qp[:, :], wB[:, c, 0:128], h[:, c, :], c == 0, c == 7, [wBt.r(), h.r(c)], [Qp.r()])
                for c in range(8):
                    MM(Gp[:, :], wB[:, c, 128:256], h[:, c, :], c == 0, c == 7, [wBt.r(), h.r(c)], [Gp.r()])
                qn = headnorm_q(Qp, der[:, 24 + j2:25 + j2], 64, dst=Bp[4 + j % 2])
                gate = Fp[6 + j % 2]
                ACT(gate[:, :], Gp[:, :], AF.Exp, [Gp.r()], [gate.r()], scale=-1.0)
                return qn, gate

            pending = prep(0)
            mem_q = None
            for j in range(6):
                qn, gate = pending
                if j + 1 < 6:
                    pending = prep(j + 1)
                else:
                    mem_q = mem_prep(l, (_View(Fp[0]), _View(Fp[1])))

                def aug(e, c0, j=j):
                    hh = 2 * j + e
                    base = 32 * (hh % 3)
                    grp = hh // 3
                    return onesb[base:base + 32, :], Dh[base:base + 32, grp, g * 512 + c0:(g + 1) * 512]

                attn_core(qn[:, :], qn.r(),
                          lambda e, j=j: kT[64 * e:64 * e + 64, j, :], 4 * g + 4,
                          lambda kb, e, j=j: vv[:, kb, (2 * j + e) * 64:(2 * j + e) * 64 + 64], vv.r(), kT.r(),
                          mixed[:, j, :], mixed.r(j), gate=gate[:, :], gatereg=gate.r(),
                          aug=aug, bias=lambda kb, e, j=j: negD[:, kb, 2 * j + e:2 * j + e + 1], causal_g=g, acc=accs[j % 2])
            rotn[0] = 6
            return mem_q

        def kv_stage(g):
            norm_x(V_KVN)
            Wk = wrows(wkv_d)
            kw = []
            for (c0, n) in ((0, 4), (4, 2)):
                kw.append(loadw(Wk[:, :, c0 * 128:(c0 + n) * 128], (8, n * 128)))

            def kproj(j):
                k_v, k_t = kw[0] if j < 4 else kw[1]
                jj = j if j < 4 else j - 4
                Kp = rot()
                for c in range(8):
                    MM(Kp[:, :], k_v[:, c, jj * 128:(jj + 1) * 128], h[:, c, :], c == 0, c == 7, [k_t.r(), h.r(c)], [Kp.r()])
                return Kp

            def kfin(j, Kp):
                sq = Bt()
                ACT(sq[:, :], Kp[:, :], AF.Square, [Kp.r()], [sq.r()])
                ns = rot()
                MM(ns[:, :], bd64, sq[:, :], True, True, [sq.r()] + CR, [ns.r()])
                rs = Ft()
                rsqrt_from(ns[:, :], 1.0 / 64, [ns.r()], [rs.r()], rs[:, :])
                STT(kT[:, j, g * 512:(g + 1) * 512], Kp[:, :], vecs[:, V_FKG:V_FKG + 1], rs[:, :], ALU.mult, ALU.mult,
                    [Kp.r(), rs.r()] + CR, [kT.r()])

            prevK = kproj(0)
            for j in range(6):
                nxtK = kproj(j + 1) if j + 1 < 6 else None
                kfin(j, prevK)
                prevK = nxtK
            for (c0, n) in ((0, 512), (512, 256)):
                v_v, v_t = loadw(Wk[:, :, 768 + c0:768 + c0 + n], (8, n))
                for tb in range(4):
                    Vp = rot()
                    for c in range(8):
                        MM(Vp[:, 0:n], h[:, c, tb * 128:(tb + 1) * 128], v_v[:, c, :], c == 0, c == 7, [v_t.r(), h.r(c)], [Vp.r()])
                    CP(vv[:, 4 * g + tb, c0:c0 + n], Vp[:, 0:n], [Vp.r()], [vv.r()], eng=("act" if tb % 2 else "dve"))
            f_v, f_t = loadw(Wk[:, :, 1536:1548], (8, 12))
            FL = rot()
            for c in range(8):
                MM(FL[0:12, :], f_v[:, c, :], h[:, c, :], c == 0, c == 7, [f_t.r(), h.r(c)], [FL.r()])
            ee, Dg, r1, r2 = Fp[0:4]
            ACT(ee[0:12, :], FL[0:12, :], AF.Exp, [FL.r()] + CR, [ee.r()], bias=der[0:12, 30:31], scale=-1.0)
            ACT(ee[0:12, :], ee[0:12, :], AF.Ln, [ee.r()], [ee.r()], bias=1.0)
            for q4 in range(4):
                ini = carry[0:12, 0:1] if q4 == 0 else Dg[0:12, q4 * 128 - 1:q4 * 128]
                k.op("dve", lambda e, q4=q4, ini=ini: e.tensor_tensor_scan(Dg[0:12, q4 * 128:(q4 + 1) * 128], onesb[0:12, :],
                                                                           ee[0:12, q4 * 128:(q4 + 1) * 128], ini, ALU.mult, ALU.subtract),
                     [ee.r(), carry.r(), Dg.r()] + CR, [Dg.r()])
            CP(carry[0:12, 0:1], Dg[0:12, 511:512], [Dg.r()], [carry.r()])
            for tb in range(4):
                tp = rot()
                TR(tp[:, 0:12], Dg[0:12, tb * 128:(tb + 1) * 128], ident[0:12, 0:12], [Dg.r()] + CR, [tp.r()])
                TS(negD[:, 4 * g + tb, :], tp[:, 0:12], -1.0, None, ALU.mult, None, [tp.r()], [negD.r()])
            hi, mid, lo = Bp[0:3]
            CP(hi[0:12, :], Dg[0:12, :], [Dg.r()], [hi.r()])
            TT(r1[0:12, :], Dg[0:12, :], hi[0:12, :], ALU.subtract, [Dg.r(), hi.r()], [r1.r()])
            CP(mid[0:12, :], r1[0:12, :], [r1.r()], [mid.r()])
            TT(r2[0:12, :], r1[0:12, :], mid[0:12, :], ALU.subtract, [r1.r(), mid.r()], [r2.r()])
            CP(lo[0:12, :], r2[0:12, :], [r2.r()], [lo.r()])
            parts = (hi, mid, lo)
            for grp in range(4):
                DP = rot()
                for r in range(3):
                    cs = CB_SEL + (r * 4 + grp) * 128
                    MM(DP[:, :], cstb[0:12, cs:cs + 128], parts[r][0:12, :], r == 0, r == 2, [parts[r].r()] + CR, [DP.r()])
                CP(Dh[:, grp, g * 512:(g + 1) * 512], DP[:, :], [DP.r()], [Dh.r()], eng=("act" if grp % 2 else "dve"))

        def load_x(g):
            for tb in range(4):
                r0 = (g * 4 + tb) * 128
                DMA("sp", xin[:, :], x_d[r0:r0 + 128, :], [], [xin.r()])
                for half in range(2):
                    bk = rot()
                    for c4 in range(4):
                        c = half * 4 + c4
                        TR(bk[:, c4 * 128:(c4 + 1) * 128], xin[:, c * 128:(c + 1) * 128], ident, [xin.r()] + CR, [bk.r()])
                    for c4 in range(4):
                        c = half * 4 + c4
                        CP(xg[:, c, tb * 128:(tb + 1) * 128], bk[:, c4 * 128:(c4 + 1) * 128], [bk.r()], [xg.r(c)],
                           eng=("act" if c4 % 2 else "dve"))

        outregs = []

        def store_x(g):
            for tb in range(4):
                r0 = (g * 4 + tb) * 128
                for half in range(2):
                    bk = rot()
                    for c4 in range(4):
                        c = half * 4 + c4
                        TR(bk[:, c4 * 128:(c4 + 1) * 128], xg[:, c, tb * 128:(tb + 1) * 128], ident, [xg.r(c)] + CR, [bk.r()])
                    CP(xout[:, half * 512:(half + 1) * 512], bk[:, :], [bk.r()], [xout.r()], eng=("act" if half else "dve"))
                DMA("sp", out_d[r0:r0 + 128, :], xout[:, :], [xout.r()], [])

        for g in range(n_groups):
            if PH("x"):
                load_x(g)
            for l in layers:
                if PH("ffn1"):
                    ffn(l, 0)
                if PH("norm"):
                    norm_x(V_MIX + 8 * l)
                mq = None
                if PH("mixer"):
                    if l < 2:
                        hgrn(l, g)
                    else:
                        mq = fox(l, g)
                if PH("mem"):
                    mem_attn(l, g, mq)
                if PH("wout"):
                    w_out(l)
                if PH("ffn2"):
                    ffn(l, 1)
                if l == 1 and want_kv:
                    kv_stage(g)
            if PH("x"):
                store_x(g)
        k.op("sp", None, [], [xout.r()])
        k.emit(st)
    return nc


_NC_CACHE = {}


def prep_inputs(inp, b):
    import ml_dtypes
    cf, cb = host_consts()
    wa = inp["w_in_a"]
    wb = inp["w_in_b"]
    ia = []
    for hd in range(6):
        for w in range(4):
            ia.extend(range(w * 768 + hd * 128, w * 768 + hd * 128 + 128))
    ia.extend(range(3072, 3328))
    ib = []
    for j in range(6):
        for w in range(2):
            ib.extend(range(w * 768 + j * 128, w * 768 + j * 128 + 128))
    ib.extend(range(1536, 1792))
    return cf, cb, np.ascontiguousarray(wa[:, :, ia]), np.ascontiguousarray(wb[:, :, ib])


def kernel(**inp):
    inp = {k_: np.asarray(v) for k_, v in inp.items()}
    B = inp["x"].shape[0]
    if "nc" not in _NC_CACHE:
        _NC_CACHE["nc"] = build()
    nc = _NC_CACHE["nc"]
    cf, cb, wina, winb = prep_inputs(inp, 0)
    vecs = host_vecs(inp)
    shared = {
        "vecs": vecs, "cstf": cf, "cstb": cb,
        "ffn1_w_gate": inp["ffn1_w_gate"], "ffn2_w_gate": inp["ffn2_w_gate"],
        "ffn1_w_up": inp["ffn1_w_up"], "ffn2_w_up": inp["ffn2_w_up"],
        "ffn1_w_down": inp["ffn1_w_down"], "ffn2_w_down": inp["ffn2_w_down"],
        "w_mem_kv": inp["w_mem_kv"], "wina": wina, "winb": winb, "w_kv": inp["w_kv"], "w_out": inp["w_out"],
    }
    in_maps = []
    for b in range(B):
        m = dict(shared)
        m["x"] = np.ascontiguousarray(inp["x"][b])
        m["mem"] = np.ascontiguousarray(inp["mem"][b])
        in_maps.append(m)
    res = run_bass_kernel_spmd(nc, in_maps, core_ids=list(range(B)))
    return np.stack([np.asarray(r["out"]) for r in res.results], axis=0).astype(np.float32)
```

```python
import contextlib
import numpy as np
import concourse.bass as bass
import concourse.mybir as mybir
from concourse.bass_utils import run_bass_kernel_spmd

F32 = mybir.dt.float32
BF16 = mybir.dt.bfloat16
AF = mybir.ActivationFunctionType
ALU = mybir.AluOpType


class Reg:
    __slots__ = ("w", "r", "excl")

    def __init__(self):
        self.w = []
        self.r = []
        self.excl = False


class Tile:
    def __init__(self, t):
        self.t = t
        self.regs = {}

    def r(self, key=None):
        g = self.regs.get(key)
        if g is None:
            g = self.regs[key] = Reg()
        return g

    def __getitem__(self, idx):
        return self.t[idx]


class Op:
    __slots__ = ("eng", "fn", "deps", "signal", "sigval", "dma", "dsem", "dval", "dprev")


class Sched:
    ENGS = ("pe", "act", "dve", "pool", "sp")

    def __init__(self, nc, ndma=12):
        self.nc = nc
        self.ops = []
        self.byeng = {e: [] for e in self.ENGS}
        self.ndma = ndma
        self.dma_count = {e: 0 for e in self.ENGS}
        self.dma_semcount = {}

    def _record(self, eng, fn, reads, writes, dma):
        i = len(self.ops)
        o = Op()
        o.eng = eng
        o.fn = fn
        o.signal = False
        o.sigval = 0
        o.dma = dma
        o.dsem = None
        o.dval = 0
        o.dprev = 0
        deps = set()
        for r in reads:
            deps.update(r.w)
            if r.excl:
                deps.update(d for d in r.r if self.ops[d].eng != eng)
        for w in writes:
            deps.update(w.w)
            deps.update(w.r)
        for r in reads:
            r.r.append(i)
        for w in writes:
            w.w = [i]
            w.r = []
        red = {}
        dd = []
        for d in deps:
            do = self.ops[d]
            if do.dma:
                dd.append(d)
            else:
                if do.eng == "pe" and eng == "pe" and not dma:
                    continue
                if red.get(do.eng, -1) < d:
                    red[do.eng] = d
        o.deps = dd + list(red.values())
        for d in o.deps:
            self.ops[d].signal = True
        if dma:
            n = self.dma_count[eng]
            self.dma_count[eng] = n + 1
            key = (eng, n % self.ndma)
            c = self.dma_semcount.get(key, 0)
            o.dsem = key
            o.dprev = 16 * c
            o.dval = 16 * (c + 1)
            self.dma_semcount[key] = c + 1
        self.ops.append(o)
        self.byeng[eng].append(i)
        return i

    def op(self, eng, fn, reads=(), writes=()):
        return self._record(eng, fn, reads, writes, False)

    def dma(self, eng, fn, reads=(), writes=()):
        return self._record(eng, fn, reads, writes, True)

    def emit(self, stack):
        nc = self.nc
        esem = {e: stack.enter_context(nc.semaphore("s_" + e)) for e in self.ENGS}
        dsem = {}
        for key in self.dma_semcount:
            dsem[key] = stack.enter_context(nc.semaphore("d_%s%d" % key))
        for e in self.ENGS:
            c = 0
            for i in self.byeng[e]:
                o = self.ops[i]
                if o.signal and not o.dma:
                    c += 1
                    o.sigval = c
        block = stack.enter_context(nc.Block())
        ops = self.ops

        def run(e, h):
            wm = {}
            for i in self.byeng[e]:
                o = ops[i]
                waits = {}
                for d in o.deps:
                    do = ops[d]
                    if do.dma:
                        s, v = dsem[do.dsem], do.dval
                    else:
                        s, v = esem[do.eng], do.sigval
                    k = id(s)
                    if waits.get(k, (0,))[0] < v:
                        waits[k] = (v, s)
                if o.dma and o.dprev > 0:
                    s = dsem[o.dsem]
                    k = id(s)
                    if waits.get(k, (0,))[0] < o.dprev:
                        waits[k] = (o.dprev, s)
                for k, (v, s) in waits.items():
                    if wm.get(k, 0) >= v:
                        continue
                    h.wait_ge(s, v)
                    wm[k] = v
                if o.fn is None:
                    continue
                ins = o.fn(h)
                if o.dma:
                    ins.then_inc(dsem[o.dsem], 16)
                elif o.signal:
                    ins.then_inc(esem[e], 1)

        @block.tensor
        def _(h):
            run("pe", h)

        @block.scalar
        def _(h):
            run("act", h)

        @block.vector
        def _(h):
            run("dve", h)

        @block.gpsimd
        def _(h):
            run("pool", h)

        @block.sync
        def _(h):
            run("sp", h)


S, D, NL, TG, NG = 2048, 1024, 4, 512, 4
DFF = 2816
EPS = 1e-6
NEG = -30000.0

V_FFN1, V_MIX, V_FFN2, V_MEMN, V_KVN = 0, 32, 64, 96, 128
V_OG, V_LB, V_FQG, V_FKG, V_MQG, V_MKG, V_FB = 136, 138, 150, 152, 153, 157, 161
NV = 162
CF_ID, CF_CNEG, CF_CMASK = 0, 128, 256
NCF = 768
CB_ID, CB_TRI, CB_BD, CB_ONES, CB_SEL = 0, 128, 256, 384, 512
NCB = 512 + 12 * 128


def host_consts():
    import ml_dtypes
    cf = np.zeros((128, NCF), np.float32)
    cf[:, CF_ID:CF_ID + 128] = np.eye(128, dtype=np.float32)
    s = np.arange(128)[:, None]
    t = np.arange(128)[None, :]
    cf[:, CF_CNEG:CF_CNEG + 128] = np.where(t >= s, 0.0, NEG)
    cm = np.ones((128, 512), np.float32)
    cm[:, 0::64] = 0.0
    cf[:, CF_CMASK:CF_CMASK + 512] = cm
    cb = np.zeros((128, NCB), np.float32)
    cb[:, CB_ID:CB_ID + 128] = np.eye(128)
    cb[:, CB_TRI:CB_TRI + 128] = ((t >= s) & ((t // 64) == (s // 64)))
    cb[:, CB_BD:CB_BD + 128] = ((t // 64) == (s // 64))
    cb[:, CB_ONES:CB_ONES + 128] = 1.0
    for r in range(3):
        for grp in range(4):
            for i in range(3):
                h = 3 * grp + i
                cb[h, CB_SEL + (r * 4 + grp) * 128 + 32 * i + r] = 1.0
    return cf, cb.astype(ml_dtypes.bfloat16)


def host_vecs(inp):
    v = np.zeros((128, NV), np.float32)

    def fm(a):
        return np.ascontiguousarray(a.reshape(-1, 128).T)

    for l in range(NL):
        v[:, V_FFN1 + 8 * l:V_FFN1 + 8 * l + 8] = fm(inp["ffn1_norm"][l])
        v[:, V_MIX + 8 * l:V_MIX + 8 * l + 8] = fm(inp["mix_norm"][l])
        v[:, V_FFN2 + 8 * l:V_FFN2 + 8 * l + 8] = fm(inp["ffn2_norm"][l])
        v[:, V_MEMN + 8 * l:V_MEMN + 8 * l + 8] = fm(inp["mem_norm"][l])
        v[:, V_MQG + l] = np.tile(inp["mem_q_gain"][l], 2)
        v[:, V_MKG + l] = np.tile(inp["mem_k_gain"][l], 2)
    v[:, V_KVN:V_KVN + 8] = fm(inp["kv_norm"])
    for l in range(2):
        v[:, V_OG + l] = inp["hgrn_o_gain"][l]
        v[:, V_LB + 6 * l:V_LB + 6 * l + 6] = fm(inp["hgrn_lb_logits"][l])
        v[:, V_FQG + l] = np.tile(inp["fox_q_gain"][l], 2)
    v[:, V_FKG] = np.tile(inp["fox_k_gain"], 2)
    v[0:12, V_FB] = inp["fox_f_bias"]
    return v


def build(n_groups=NG, layers=(0, 1, 2, 3), want_kv=True, phases=None, shrink=False):
    NLW = 1 if shrink else NL
    PH = (lambda p: True) if phases is None else (lambda p: p in phases)
    nc = bass.Bass("TRN2", target_bir_lowering=False)

    def din(name, shape, dt=F32):
        return nc.dram_tensor(name, list(shape), dt, kind="ExternalInput").ap()

    x_d = din("x", [S, D])
    mem_d = din("mem", [256, D])
    vecs_d = din("vecs", [128, NV])
    cstf_d = din("cstf", [128, NCF])
    cstb_d = din("cstb", [128, NCB], BF16)
    wg_d = [din("ffn1_w_gate", [NLW, D, DFF]), din("ffn2_w_gate", [NLW, D, DFF])]
    wu_d = [din("ffn1_w_up", [NLW, D, DFF]), din("ffn2_w_up", [NLW, D, DFF])]
    wd_d = [din("ffn1_w_down", [NLW, DFF, D]), din("ffn2_w_down", [NLW, DFF, D])]
    wmem_d = din("w_mem_kv", [NLW, D, 512])
    wina_d = din("wina", [2, D, 3328])
    winb_d = din("winb", [2, D, 1792])
    wkv_d = din("w_kv", [D, 1548])
    wo_d = din("w_out", [NLW, D, D])
    out_d = nc.dram_tensor("out", [S, D], F32, kind="ExternalOutput").ap()

    with contextlib.ExitStack() as st:
        k = Sched(nc)

        def sb(name, shape, dt):
            return Tile(st.enter_context(nc.sbuf_tensor("s_" + name, list(shape), dt)))

        banks = [Tile(st.enter_context(nc.psum_tensor("pb%d" % i, [128, 512], F32))) for i in range(8)]
        for b_ in banks:
            b_.r().excl = True
        rotc = [0]
        rotn = [6]

        def rot():
            b = banks[rotc[0] % rotn[0]]
            rotc[0] += 1
            return b

        NUM, DEN = banks[6], banks[7]

        xg = sb("xg", [128, 8, TG], F32)
        h = sb("h", [128, 8, TG], BF16)
        act = sb("act", [128, 11, TG], BF16)
        mixed = sb("mixed", [128, 8, TG], BF16)
        kT = sb("kT", [128, 6, S], BF16)
        vv = sb("vv", [128, 16, 768], BF16)
        Dh = sb("Dh", [128, 4, S], BF16)
        negD = sb("negD", [128, 16, 12], F32)
        memh = sb("memh", [128, 8, 256], BF16)
        kmT = sb("kmT", [128, NL, 2, 256], BF16)
        vm = sb("vm", [128, NL, 2, 256], BF16)
        vecs = sb("vecs", [128, NV], F32)
        cstf = sb("cstf", [128, NCF], F32)
        cstb = sb("cstb", [128, NCB], BF16)
        der = sb("der", [128, 64], F32)
        fac = sb("fac", [128, 8, 8], F32)
        Sst = sb("Sst", [128, 2, 6, 128], F32)
        carry = sb("carry", [128, 1], F32)
        xin = sb("xin", [128, D], F32)
        xout = xin
        NSLOT = 4
        slots = [sb("w%d" % i, [128, 4096], BF16) for i in range(NSLOT)]
        NF, NB = 10, 10
        Fp = [sb("F%d" % i, [128, 512], F32) for i in range(NF)]
        Bp = [sb("B%d" % i, [128, 512], BF16) for i in range(NB)]
        Sb2 = [sb("Sb%d" % i, [128, 128], BF16) for i in range(8)]
        Tm2 = [sb("Tm%d" % i, [128, 128], F32) for i in range(3)]
        Xb = [sb("Xb%d" % i, [128, 512], BF16) for i in range(4)]
        fc_, bc_, sc_, tc_ = [0], [0], [0], [0]

        def Ft():
            t = Fp[8 + fc_[0] % 2]
            fc_[0] += 1
            return t

        def Bt():
            t = Xb[2 + bc_[0] % 2]
            bc_[0] += 1
            return t

        ptc_ = [0]

        def PTt():
            t = Bp[ptc_[0] % 4]
            ptc_[0] += 1
            return t

        def Tmt():
            t = Tm2[tc_[0] % 3]
            tc_[0] += 1
            return t

        ident = cstf[:, CF_ID:CF_ID + 128]
        cneg = cstf[:, CF_CNEG:CF_CNEG + 128]
        cmask = cstf[:, CF_CMASK:CF_CMASK + 512]
        identb = cstb[:, CB_ID:CB_ID + 128]
        trimask = cstb[:, CB_TRI:CB_TRI + 128]
        bd64 = cstb[:, CB_BD:CB_BD + 128]
        onesb = cstb[:, CB_ONES:CB_ONES + 128]
        CR = [cstf.r(), cstb.r(), vecs.r(), der.r()]

        def MM(out, lhsT, rhs, start, stop, R, W, skip=False):
            if skip:
                k.op("pe", lambda e: e.matmul(out, lhsT, rhs, start=start, stop=stop, skip_group_check=True), R, W)
            else:
                k.op("pe", lambda e: e.matmul(out, lhsT, rhs, start=start, stop=stop), R, W)

        def TR(out, in_, idn, R, W):
            k.op("pe", lambda e: e.transpose(out, in_, idn), R, W)

        def ACT(out, in_, func, R, W, bias=None, scale=None):
            kw = {}
            if bias is not None:
                kw["bias"] = bias
            if scale is not None:
                kw["scale"] = scale
            k.op("act", lambda e: e.activation(out, in_, func, **kw), R, W)

        def TT(out, in0, in1, op, R, W, eng="dve"):
            k.op(eng, lambda e: e.tensor_tensor(out, in0, in1, op), R, W)

        def TS(out, in0, s1, s2, op0, op1, R, W, eng="dve"):
            if s2 is None:
                k.op(eng, lambda e: e.tensor_scalar(out, in0, s1, None, op0), R, W)
            else:
                k.op(eng, lambda e: e.tensor_scalar(out, in0, s1, s2, op0, op1), R, W)

        def STT(out, in0, sc, in1, op0, op1, R, W):
            k.op("dve", lambda e: e.scalar_tensor_tensor(out, in0, sc, in1, op0, op1), R, W)

        def CP(out, in_, R, W, eng="dve"):
            if eng == "act":
                k.op("act", lambda e: e.activation(out, in_, AF.Copy), R, W)
            else:
                k.op(eng, lambda e: e.tensor_copy(out, in_), R, W)

        def RECIP(out, in_, R, W):
            k.op("dve", lambda e: e.reciprocal(out, in_), R, W)

        def DMA(eng, out, in_, R, W):
            k.dma(eng, lambda e: e.dma_start(out=out, in_=in_), R, W)

        slotc = [0]

        def loadw(src, shape):
            t = slots[slotc[0] % NSLOT]
            slotc[0] += 1
            a, b = shape
            view = t[:, 0:a * b].rearrange("p (a b) -> p a b", a=a)
            DMA("pool", view, src, [], [t.r()])
            return view, t

        def wrows(w2d):
            return w2d.rearrange("(k p) n -> p k n", p=128)

        def rsqrt_from(ps_ap, scale, R_ps, W_t, out_ap):
            ACT(out_ap, ps_ap, AF.Ln, R_ps, W_t, bias=EPS, scale=scale)
            ACT(out_ap, out_ap, AF.Exp, W_t, W_t, scale=-0.5)

        DMA("sp", cstf[:, :], cstf_d, [], [cstf.r()])
        DMA("sp", cstb[:, :], cstb_d, [], [cstb.r()])
        DMA("sp", vecs[:, :], vecs_d, [], [vecs.r()])
        if PH("der"):
            k.op("dve", lambda e: e.memset(der[:, 0:6], 0.0), [], [der.r()])
            k.op("dve", lambda e: e.memset(der[:, 12:18], 1.0), [der.r()], [der.r()])
            TT(der[:, 6:12], vecs[:, V_LB + 6:V_LB + 12], vecs[:, V_LB:V_LB + 6], ALU.subtract, [vecs.r(), der.r()], [der.r()])
            ACT(der[:, 6:12], der[:, 6:12], AF.Sigmoid, [der.r()], [der.r()])
            TS(der[:, 18:24], der[:, 6:12], -1.0, 1.0, ALU.mult, ALU.add, [der.r()], [der.r()])
            TS(der[:, 24:26], vecs[:, V_FQG:V_FQG + 2], 0.125, None, ALU.mult, None, [vecs.r(), der.r()], [der.r()])
            TS(der[:, 26:30], vecs[:, V_MQG:V_MQG + 4], 0.125, None, ALU.mult, None, [vecs.r(), der.r()], [der.r()])
            TS(der[:, 30:31], vecs[:, V_FB:V_FB + 1], -1.0, None, ALU.mult, None, [vecs.r(), der.r()], [der.r()])
            TS(der[:, 32:44], der[:, 12:24], 0.5, None, ALU.mult, None, [der.r()], [der.r()])
            TT(der[:, 44:56], der[:, 32:44], der[:, 0:12], ALU.add, [der.r()], [der.r()])
            k.op("dve", lambda e: e.memset(Sst[:, :, :, :], 0.0), [], [Sst.r((l, hd)) for l in range(2) for hd in range(6)])
            k.op("dve", lambda e: e.memset(carry[:, :], 0.0), [], [carry.r()])

        def LB(l, hd):
            return der[:, 6 * l + hd:6 * l + hd + 1]

        def OML(l, hd):
            return der[:, 12 + 6 * l + hd:12 + 6 * l + hd + 1]

        def HOML(l, hd):
            return der[:, 32 + 6 * l + hd:32 + 6 * l + hd + 1]

        def HB(l, hd):
            return der[:, 44 + 6 * l + hd:44 + 6 * l + hd + 1]

        memT = [Fp[0], Fp[1], Fp[2], Fp[3]]

        def memT_ap(c):
            return memT[c // 2][:, (c % 2) * 256:(c % 2) * 256 + 256]

        for tb in (range(2) if PH("setup_mem") else ()):
            DMA("sp", xin[:, :], mem_d[tb * 128:(tb + 1) * 128, :], [], [xin.r()])
            for half in range(2):
                bk = rot()
                for c4 in range(4):
                    c = half * 4 + c4
                    TR(bk[:, c4 * 128:(c4 + 1) * 128], xin[:, c * 128:(c + 1) * 128], ident, [xin.r()] + CR, [bk.r()])
                for c4 in range(4):
                    c = half * 4 + c4
                    CP(memT_ap(c)[:, tb * 128:(tb + 1) * 128], bk[:, c4 * 128:(c4 + 1) * 128], [bk.r()], [memT[c // 2].r()])
        nsb = rot()
        for c in (range(8) if PH("setup_mem") else ()):
            sq = Bt()
            ACT(sq[:, 0:256], memT_ap(c), AF.Square, [memT[c // 2].r()], [sq.r()])
            MM(nsb[:, 0:256], onesb, sq[:, 0:256], c == 0, c == 7, [sq.r()] + CR, [nsb.r()])
        rsm = Fp[4]
        if PH("setup_mem"):
            rsqrt_from(nsb[:, 0:256], 1.0 / D, [nsb.r()], [rsm.r()], rsm[:, 0:256])
        for c in (range(8) if PH("setup_mem") else ()):
            TT(memh[:, c, :], memT_ap(c), rsm[:, 0:256], ALU.mult, [memT[c // 2].r(), rsm.r()], [memh.r()])
        memn = act
        for l in (range(NL) if PH("setup_mem") else ()):
            for c in range(8):
                TS(memn[:, c, 0:256], memh[:, c, :], vecs[:, V_MEMN + 8 * l + c:V_MEMN + 8 * l + c + 1], None, ALU.mult, None,
                   [memh.r()] + CR, [act.r()])
            wm, wmt = loadw(wrows(wmem_d[min(l, NLW - 1)]), (8, 512))
            for c2 in range(2):
                kp = rot()
                for c in range(8):
                    MM(kp[:, 0:256], wm[:, c, c2 * 128:(c2 + 1) * 128], memn[:, c, 0:256], c == 0, c == 7, [wmt.r(), act.r()], [kp.r()])
                sq = Bt()
                ACT(sq[:, 0:256], kp[:, 0:256], AF.Square, [kp.r()], [sq.r()])
                ns = rot()
                MM(ns[:, 0:256], bd64, sq[:, 0:256], True, True, [sq.r()] + CR, [ns.r()])
                rs = Ft()
                rsqrt_from(ns[:, 0:256], 1.0 / 64, [ns.r()], [rs.r()], rs[:, 0:256])
                STT(kmT[:, l, c2, :], kp[:, 0:256], vecs[:, V_MKG + l:V_MKG + l + 1], rs[:, 0:256], ALU.mult, ALU.mult,
                    [kp.r(), rs.r()] + CR, [kmT.r(l)])
            for tb in range(2):
                vp = rot()
                for c in range(8):
                    MM(vp[:, 0:256], memn[:, c, tb * 128:(tb + 1) * 128], wm[:, c, 256:512], c == 0, c == 7, [wmt.r(), act.r()], [vp.r()])
                CP(vm[:, l, tb, :], vp[:, 0:256], [vp.r()], [vm.r(l)])

        XR = [xg.r(c) for c in range(8)]

        def norm_x(gcol):
            ns = rot()
            for c in range(8):
                sq = Bt()
                ACT(sq[:, :], xg[:, c, :], AF.Square, [xg.r(c)], [sq.r()])
                MM(ns[:, :], onesb, sq[:, :], c == 0, c == 7, [sq.r()] + CR, [ns.r()])
            rs = Ft()
            rsqrt_from(ns[:, :], 1.0 / D, [ns.r()], [rs.r()], rs[:, :])
            for c in range(8):
                eng = "dve"
                k.op(eng, lambda e, c=c: e.scalar_tensor_tensor(h[:, c, :], xg[:, c, :], vecs[:, gcol + c:gcol + c + 1], rs[:, :],
                                                                ALU.mult, ALU.mult),
                     [xg.r(c), rs.r()] + CR, [h.r(c)])

        def ffn(l, which):
            norm_x((V_FFN1 if which == 0 else V_FFN2) + 8 * l)
            lw = min(l, NLW - 1)
            Wg, Wu, Wd = wrows(wg_d[which][lw]), wrows(wu_d[which][lw]), wd_d[which][lw].rearrange("(c p) d -> p c d", p=128)
            for half in range(2):
                f0 = half * 11
                fi = 0
                for nch in (4, 4, 3):
                    c0 = (f0 + fi) * 128
                    g_v, g_t = loadw(Wg[:, :, c0:c0 + nch * 128], (8, nch * 128))
                    u_v, u_t = loadw(Wu[:, :, c0:c0 + nch * 128], (8, nch * 128))
                    for j in range(nch):
                        G, U = rot(), rot()
                        for c in range(8):
                            MM(G[:, :], g_v[:, c, j * 128:(j + 1) * 128], h[:, c, :], c == 0, c == 7, [g_t.r(), h.r(c)], [G.r()])
                        for c in range(8):
                            MM(U[:, :], u_v[:, c, j * 128:(j + 1) * 128], h[:, c, :], c == 0, c == 7, [u_t.r(), h.r(c)], [U.r()])
                        sg = Ft()
                        ACT(sg[:, :], G[:, :], AF.Silu, [G.r()], [sg.r()])
                        TT(act[:, fi, :], sg[:, :], U[:, :], ALU.mult, [sg.r(), U.r()], [act.r()])
                        fi += 1
                for db in range(4):
                    d_v, d_t = loadw(Wd[:, f0:f0 + 11, db * 256:(db + 1) * 256], (11, 256))
                    for dc in range(2):
                        d = db * 2 + dc
                        Y = rot()
                        for f in range(11):
                            MM(Y[:, :], d_v[:, f, dc * 128:(dc + 1) * 128], act[:, f, :], f == 0, f == 10, [d_t.r(), act.r()], [Y.r()])
                        STT(xg[:, d, :], Y[:, :], 0.5, xg[:, d, :], ALU.mult, ALU.add, [Y.r(), xg.r(d)], [xg.r(d)])

        def attn_core(q, qreg, kT_ap, nkb, v_ap, vreg, kreg, out_ap, outreg, gate=None, gatereg=None,
                      aug=None, bias=None, causal_g=None, acc=None):
            NUM, DEN = acc if acc is not None else (banks[6], banks[7])
            def stageA(kb):
                res = []
                diag = causal_g is not None and kb >= 4 * causal_g
                c0 = (kb - 4 * causal_g) * 128 if diag else 0
                Ss = [rot(), rot()]
                for e in range(2):
                    rows = slice(64 * e, 64 * e + 64)
                    MM(Ss[e][:, c0:512], kT_ap(e)[:, kb * 128:(kb + 1) * 128], q[rows, c0:512], True, aug is None,
                       [kreg, qreg], [Ss[e].r()])
                if aug is not None:
                    for e in range(2):
                        o1, d1 = aug(e, c0)
                        MM(Ss[e][:, c0:512], o1, d1, False, True, [Dh.r()] + CR, [Ss[e].r()])
                for e in range(2):
                    Sb_ = Ss[e]
                    PT = PTt()
                    bi = bias(kb, e) if bias is not None else None
                    br = [negD.r()] if bias is not None else []
                    if diag:
                        tm = Tmt()
                        TT(tm[:, :], Sb_[:, c0:c0 + 128], cneg, ALU.add, [Sb_.r()] + CR, [tm.r()])
                        ACT(PT[:, c0:c0 + 128], tm[:, :], AF.Exp, [tm.r()] + br, [PT.r()], bias=bi)
                        if c0 + 128 < 512:
                            ACT(PT[:, c0 + 128:512], Sb_[:, c0 + 128:512], AF.Exp, [Sb_.r()] + br, [PT.r()], bias=bi)
                    else:
                        ACT(PT[:, c0:512], Sb_[:, c0:512], AF.Exp, [Sb_.r()] + br, [PT.r()], bias=bi)
                    res.append((PT, c0))
                return res

            def stageB(kb, res):
                for e in range(2):
                    PT, c0 = res[e]
                    rows = slice(64 * e, 64 * e + 64)
                    MM(NUM[rows, c0:512], v_ap(kb, e), PT[:, c0:512], kb == 0, kb == nkb - 1, [PT.r(), vreg], [NUM.r()])
                for e in range(2):
                    PT, c0 = res[e]
                    rows = slice(64 * e, 64 * e + 64)
                    MM(DEN[rows, c0:512], onesb[:, 0:64], PT[:, c0:512], kb == 0, kb == nkb - 1, [PT.r()] + CR, [DEN.r()])

            prev = stageA(0)
            for kb in range(nkb):
                nxt = stageA(kb + 1) if kb + 1 < nkb else None
                stageB(kb, prev)
                prev = nxt
            rec = Ft()
            if gate is not None:
                STT(rec[:, :], gate, 1.0, DEN[:, :], ALU.add, ALU.mult, [gatereg, DEN.r()], [rec.r()])
                RECIP(rec[:, :], rec[:, :], [rec.r()], [rec.r()])
            else:
                RECIP(rec[:, :], DEN[:, :], [DEN.r()], [rec.r()])
            TT(out_ap, NUM[:, :], rec[:, :], ALU.mult, [NUM.r(), rec.r()], [outreg])

        def headnorm_q(P, wq_scalar, width, dst=None):
            sq = Bt()
            ACT(sq[:, :], P[:, :], AF.Square, [P.r()], [sq.r()])
            ns = rot()
            MM(ns[:, :], bd64, sq[:, :], True, True, [sq.r()] + CR, [ns.r()])
            rs = Ft()
            rsqrt_from(ns[:, :], 1.0 / 64, [ns.r()], [rs.r()], rs[:, :])
            qn = dst if dst is not None else Bp[4]
            STT(qn[:, :], P[:, :], wq_scalar, rs[:, :], ALU.mult, ALU.mult, [P.r(), rs.r()] + CR, [qn.r()])
            return qn

        class _ActView:
            def __init__(self, i):
                self.ap = act[:, i, :]

            def r(self, key=None):
                return act.r()

            def __getitem__(self, idx):
                return self.ap[idx]

        def mem_prep(l):
            if l < 2:
                src = wrows(wina_d[l])[:, :, 3072:3328]
            else:
                src = wrows(winb_d[l - 2])[:, :, 1536:1792]
            wq, wqt = loadw(src, (8, 256))
            qns = []
            for c2 in range(2):
                QM = rot()
                for c in range(8):
                    MM(QM[:, :], wq[:, c, c2 * 128:(c2 + 1) * 128], h[:, c, :], c == 0, c == 7, [wqt.r(), h.r(c)], [QM.r()])
                qns.append(headnorm_q(QM, der[:, 26 + l:27 + l], 64, dst=_ActView(c2)))
            return qns

        def mem_attn(l, g, qns):
            rotn[0] = 4
            accs = ((banks[4], banks[5]), (banks[6], banks[7]))
            for c2 in range(2):
                qn = qns[c2]
                attn_core(qn[:, :], qn.r(),
                          lambda e, c2=c2: kmT[64 * e:64 * e + 64, l, c2, :], 2,
                          lambda kb, e, c2=c2: vm[:, l, kb, (c2 * 2 + e) * 64:(c2 * 2 + e) * 64 + 64], vm.r(l), kmT.r(l),
                          mixed[:, 6 + c2, :], mixed.r(6 + c2), acc=accs[c2])
            rotn[0] = 6

        def w_out(l):
            Wo = wrows(wo_d[min(l, NLW - 1)])
            for hb in range(2):
                o_v, o_t = loadw(Wo[:, :, hb * 512:(hb + 1) * 512], (8, 512))
                for dd in range(4):
                    d = hb * 4 + dd
                    Y = rot()
                    for c in range(8):
                        MM(Y[:, :], o_v[:, c, dd * 128:(dd + 1) * 128], mixed[:, c, :], c == 0, c == 7, [o_t.r(), mixed.r(c)], [Y.r()])
                    TT(xg[:, d, :], Y[:, :], xg[:, d, :], ALU.add, [Y.r(), xg.r(d)], [xg.r(d)])

        def hgrn(l, g):
            Wa = wrows(wina_d[l])
            th, logf, cc, dq, eq, ek, qs, t1 = Fp[0:8]
            ktT, scT = Xb[0], Xb[1]

            B_Q, B_F, B_G, B_V, B_KA, B_KB, B_O, B_T = banks

            def S1p(hd):
                par = hd % 2
                wA, wAt = loadw(Wa[:, :, hd * 512:(hd + 1) * 512], (8, 512))
                Q, Fq, Gq, Vq = B_Q, B_F, B_G, B_V
                for c in range(8):
                    MM(Q[:, :], wA[:, c, 0:128], h[:, c, :], c == 0, c == 7, [wAt.r(), h.r(c)], [Q.r()])
                for c in range(8):
                    MM(Gq[:, :], wA[:, c, 384:512], h[:, c, :], c == 0, c == 7, [wAt.r(), h.r(c)], [Gq.r()])
                for c in range(8):
                    MM(Fq[:, :], wA[:, c, 128:256], h[:, c, :], c == 0, c == 7, [wAt.r(), h.r(c)], [Fq.r()])
                for tb in range(4):
                    for c in range(8):
                        MM(Vq[:, tb * 128:(tb + 1) * 128], h[:, c, tb * 128:(tb + 1) * 128], wA[:, c, 256:384], c == 0, c == 7,
                           [wAt.r(), h.r(c)], [Vq.r()])

            def S1f(hd):
                par = hd % 2
                qt, kt, gs, vtok, kt2 = Bp[5 * par:5 * par + 5]
                Q, Fq, Gq, Vq = B_Q, B_F, B_G, B_V
                ACT(qs[:, :], Q[:, :], AF.Silu, [Q.r()], [qs.r()])
                ACT(gs[:, :], Gq[:, :], AF.Silu, [Gq.r()], [gs.r()])
                ACT(th[:, :], Fq[:, :], AF.Tanh, [Fq.r()], [th.r()], scale=0.5)
                CP(vtok[:, :], Vq[:, :], [Vq.r()], [vtok.r()], eng="act")

            def S1e(hd):
                par = hd % 2
                qt, kt, gs, vtok, kt2 = Bp[5 * par:5 * par + 5]
                fo = 4 * par
                freg = fac.r(par)
                ACT(logf[:, :], th[:, :], AF.Ln, [th.r()] + CR, [logf.r()], bias=HB(l, hd), scale=HOML(l, hd))
                k.op("dve", lambda e: e.tensor_tensor_scan(cc[:, :], cmask, logf[:, :], 0.0, ALU.mult, ALU.add),
                     [logf.r()] + CR, [cc.r()])
                TT(fac[:, fo + 3, :], cc[:, 63:512:64], cc[:, 31:512:64], ALU.subtract, [cc.r()], [freg])
                cm_b = cc[:, 31:512:64].unsqueeze(2).to_broadcast([128, 8, 64])
                TT(dq[:, :].rearrange("p (a b) -> p a b", b=64), cc[:, :].rearrange("p (a b) -> p a b", b=64), cm_b, ALU.subtract,
                   [cc.r()], [dq.r()])
                ACT(fac[:, fo + 1, :], fac[:, fo + 3, :], AF.Exp, [freg], [freg])
                ACT(ek[:, :], dq[:, :], AF.Exp, [dq.r()], [ek.r()], scale=-1.0)
                ACT(eq[:, :], dq[:, :], AF.Exp, [dq.r()], [eq.r()])
                ACT(fac[:, fo + 0, :], cc[:, 63:512:64], AF.Exp, [cc.r(), freg], [freg])
                ACT(fac[:, fo + 2, :], cc[:, 31:512:64], AF.Exp, [cc.r(), freg], [freg])
                TS(t1[:, :], th[:, :], -1.0, 1.0, ALU.mult, ALU.add, [th.r()], [t1.r()])
                STT(kt[:, :], t1[:, :], HOML(l, hd), ek[:, :], ALU.mult, ALU.mult, [t1.r(), ek.r()] + CR, [kt.r()])
                TT(kt2[:, :].rearrange("p (a b) -> p a b", b=64), kt[:, :].rearrange("p (a b) -> p a b", b=64),
                   fac[:, fo + 1, :].unsqueeze(2).to_broadcast([128, 8, 64]), ALU.mult, [kt.r(), freg], [kt2.r()])
                TT(qt[:, :], qs[:, :], eq[:, :], ALU.mult, [qs.r(), eq.r()], [qt.r()])
                return qt, kt, kt2, gs, vtok, fo, freg

            def S2a(hd, ctx):
                qt, kt, kt2, gs, vtok, fo, freg = ctx
                tp = B_T
                tpb = tp[:, 0:256].bitcast(BF16)
                for tb in range(4):
                    TR(tpb[:, tb * 128:(tb + 1) * 128], kt2[:, tb * 128:(tb + 1) * 128], identb, [kt2.r()] + CR, [tp.r()])
                CP(ktT[:, :], tpb, [tp.r()], [ktT.r()], eng="act")
                KVb = [B_KA, B_KB]
                for n in range(8):
                    b = n // 2
                    rows = slice(64 * (n % 2), 64 * (n % 2) + 64)
                    MM(KVb[n % 2][:, b * 128:(b + 1) * 128], ktT[rows, b * 128:(b + 1) * 128], vtok[rows, b * 128:(b + 1) * 128],
                       True, True, [ktT.r(), vtok.r()], [KVb[n % 2].r()])
                SC = B_T
                for b in range(4):
                    MM(SC[:, b * 128:(b + 1) * 128], kt[:, b * 128:(b + 1) * 128], qt[:, b * 128:(b + 1) * 128], True, True,
                       [kt.r(), qt.r()], [SC.r()])
                TT(scT[:, :].rearrange("p (a b) -> p a b", b=128), SC[:, :].rearrange("p (a b) -> p a b", b=128),
                   trimask.unsqueeze(1).to_broadcast([128, 4, 128]), ALU.mult, [SC.r()] + CR, [scT.r()])
                O = B_O
                for b in range(4):
                    MM(O[:, b * 128:(b + 1) * 128], vtok[:, b * 128:(b + 1) * 128], scT[:, b * 128:(b + 1) * 128], b == 0, False,
                       [vtok.r(), scT.r()], [O.r()], skip=True)

            def S2c(hd, ctx):
                qt, kt, kt2, gs, vtok, fo, freg = ctx
                KVb = [B_KA, B_KB]
                O = B_O
                sreg = Sst.r((l, hd))
                Sap = Sst[:, l, hd, :]
                sbs = []
                for n in range(8):
                    sbt = Sb2[n]
                    TS(sbt[:, :], Sap, fac[:, fo + 2, n:n + 1], None, ALU.mult, None, [sreg, freg], [sbt.r()])
                    STT(Sap, Sap, fac[:, fo + 0, n:n + 1], KVb[n % 2][:, (n // 2) * 128:(n // 2 + 1) * 128], ALU.mult, ALU.add,
                        [sreg, freg, KVb[n % 2].r()], [sreg])
                    sbs.append(sbt)
                for n in range(8):
                    MM(O[:, n * 64:(n + 1) * 64], sbs[n][:, :], qt[:, n * 64:(n + 1) * 64], False, n == 7, [sbs[n].r(), qt.r()], [O.r()], skip=True)

            def S2t(hd, ctx):
                qt, kt, kt2, gs, vtok, fo, freg = ctx
                O = B_O
                sqo = Bt()
                ACT(sqo[:, :], O[:, :], AF.Square, [O.r()], [sqo.r()])
                ns = B_T
                MM(ns[:, :], onesb, sqo[:, :], True, True, [sqo.r()] + CR, [ns.r()])
                rs = Ft()
                rsqrt_from(ns[:, :], 1.0 / 128, [ns.r()], [rs.r()], rs[:, :])
                on = Ft()
                STT(on[:, :], O[:, :], vecs[:, V_OG + l:V_OG + l + 1], rs[:, :], ALU.mult, ALU.mult, [O.r(), rs.r()] + CR, [on.r()])
                TT(mixed[:, hd, :], on[:, :], gs[:, :], ALU.mult, [on.r(), gs.r()], [mixed.r(hd)])

            S1p(0)
            S1f(0)
            S1p(1)
            pending = S1e(0)
            for hd in range(6):
                cur = pending
                S2a(hd, cur)
                if hd + 1 < 6:
                    S1f(hd + 1)
                if hd + 2 < 6:
                    S1p(hd + 2)
                S2c(hd, cur)
                if hd + 1 < 6:
                    pending = S1e(hd + 1)
                S2t(hd, cur)

        def fox(l, g):
            j2 = l - 2
            Wb = wrows(winb_d[j2])
            rotn[0] = 4
            accs = ((banks[4], banks[5]), (banks[6], banks[7]))

            def prep(j):
                wB, wBt = loadw(Wb[:, :, j * 256:(j + 1) * 256], (8, 256))
                Qp, Gp = rot(), rot()
                for c in range(8):
                    MM(Qp[:, :], wB[:, c, 0:128], h[:, c, :], c == 0, c == 7, [wBt.r(), h.r(c)], [Qp.r()])
                for c in range(8):
                    MM(Gp[:, :], wB[:, c, 128:256], h[:, c, :], c == 0, c == 7, [wBt.r(), h.r(c)], [Gp.r()])
                qn = headnorm_q(Qp, der[:, 24 + j2:25 + j2], 64, dst=Bp[4 + j % 2])
                gate = Fp[6 + j % 2]
                ACT(gate[:, :], Gp[:, :], AF.Exp, [Gp.r()], [gate.r()], scale=-1.0)
                return qn, gate

            pending = prep(0)
            for j in range(6):
                qn, gate = pending
                if j + 1 < 6:
                    pending = prep(j + 1)

                def aug(e, c0, j=j):
                    hh = 2 * j + e
                    base = 32 * (hh % 3)
                    grp = hh // 3
                    return onesb[base:base + 32, :], Dh[base:base + 32, grp, g * 512 + c0:(g + 1) * 512]

                attn_core(qn[:, :], qn.r(),
                          lambda e, j=j: kT[64 * e:64 * e + 64, j, :], 4 * g + 4,
                          lambda kb, e, j=j: vv[:, kb, (2 * j + e) * 64:(2 * j + e) * 64 + 64], vv.r(), kT.r(),
                          mixed[:, j, :], mixed.r(j), gate=gate[:, :], gatereg=gate.r(),
                          aug=aug, bias=lambda kb, e, j=j: negD[:, kb, 2 * j + e:2 * j + e + 1], causal_g=g, acc=accs[j % 2])
            rotn[0] = 6

        def kv_stage(g):
            norm_x(V_KVN)
            Wk = wrows(wkv_d)
            kw = []
            for (c0, n) in ((0, 4), (4, 2)):
                kw.append(loadw(Wk[:, :, c0 * 128:(c0 + n) * 128], (8, n * 128)))

            def kproj(j):
                k_v, k_t = kw[0] if j < 4 else kw[1]
                jj = j if j < 4 else j - 4
                Kp = rot()
                for c in range(8):
                    MM(Kp[:, :], k_v[:, c, jj * 128:(jj + 1) * 128], h[:, c, :], c == 0, c == 7, [k_t.r(), h.r(c)], [Kp.r()])
                return Kp

            def kfin(j, Kp):
                sq = Bt()
                ACT(sq[:, :], Kp[:, :], AF.Square, [Kp.r()], [sq.r()])
                ns = rot()
                MM(ns[:, :], bd64, sq[:, :], True, True, [sq.r()] + CR, [ns.r()])
                rs = Ft()
                rsqrt_from(ns[:, :], 1.0 / 64, [ns.r()], [rs.r()], rs[:, :])
                STT(kT[:, j, g * 512:(g + 1) * 512], Kp[:, :], vecs[:, V_FKG:V_FKG + 1], rs[:, :], ALU.mult, ALU.mult,
                    [Kp.r(), rs.r()] + CR, [kT.r()])

            prevK = kproj(0)
            for j in range(6):
                nxtK = kproj(j + 1) if j + 1 < 6 else None
                kfin(j, prevK)
                prevK = nxtK
            for (c0, n) in ((0, 512), (512, 256)):
                v_v, v_t = loadw(Wk[:, :, 768 + c0:768 + c0 + n], (8, n))
                for tb in range(4):
                    Vp = rot()
                    for c in range(8):
                        MM(Vp[:, 0:n], h[:, c, tb * 128:(tb + 1) * 128], v_v[:, c, :], c == 0, c == 7, [v_t.r(), h.r(c)], [Vp.r()])
                    CP(vv[:, 4 * g + tb, c0:c0 + n], Vp[:, 0:n], [Vp.r()], [vv.r()], eng=("act" if tb % 2 else "dve"))
            f_v, f_t = loadw(Wk[:, :, 1536:1548], (8, 12))
            FL = rot()
            for c in range(8):
                MM(FL[0:12, :], f_v[:, c, :], h[:, c, :], c == 0, c == 7, [f_t.r(), h.r(c)], [FL.r()])
            ee, Dg, r1, r2 = Fp[0:4]
            ACT(ee[0:12, :], FL[0:12, :], AF.Exp, [FL.r()] + CR, [ee.r()], bias=der[0:12, 30:31], scale=-1.0)
            ACT(ee[0:12, :], ee[0:12, :], AF.Ln, [ee.r()], [ee.r()], bias=1.0)
            for q4 in range(4):
                ini = carry[0:12, 0:1] if q4 == 0 else Dg[0:12, q4 * 128 - 1:q4 * 128]
                k.op("dve", lambda e, q4=q4, ini=ini: e.tensor_tensor_scan(Dg[0:12, q4 * 128:(q4 + 1) * 128], onesb[0:12, :],
                                                                           ee[0:12, q4 * 128:(q4 + 1) * 128], ini, ALU.mult, ALU.subtract),
                     [ee.r(), carry.r(), Dg.r()] + CR, [Dg.r()])
            CP(carry[0:12, 0:1], Dg[0:12, 511:512], [Dg.r()], [carry.r()])
            for tb in range(4):
                tp = rot()
                TR(tp[:, 0:12], Dg[0:12, tb * 128:(tb + 1) * 128], ident[0:12, 0:12], [Dg.r()] + CR, [tp.r()])
                TS(negD[:, 4 * g + tb, :], tp[:, 0:12], -1.0, None, ALU.mult, None, [tp.r()], [negD.r()])
            hi, mid, lo = Bp[0:3]
            CP(hi[0:12, :], Dg[0:12, :], [Dg.r()], [hi.r()])
            TT(r1[0:12, :], Dg[0:12, :], hi[0:12, :], ALU.subtract, [Dg.r(), hi.r()], [r1.r()])
            CP(mid[0:12, :], r1[0:12, :], [r1.r()], [mid.r()])
            TT(r2[0:12, :], r1[0:12, :], mid[0:12, :], ALU.subtract, [r1.r(), mid.r()], [r2.r()])
            CP(lo[0:12, :], r2[0:12, :], [r2.r()], [lo.r()])
            parts = (hi, mid, lo)
            for grp in range(4):
                DP = rot()
                for r in range(3):
                    cs = CB_SEL + (r * 4 + grp) * 128
                    MM(DP[:, :], cstb[0:12, cs:cs + 128], parts[r][0:12, :], r == 0, r == 2, [parts[r].r()] + CR, [DP.r()])
                CP(Dh[:, grp, g * 512:(g + 1) * 512], DP[:, :], [DP.r()], [Dh.r()], eng=("act" if grp % 2 else "dve"))

        def load_x(g):
            for tb in range(4):
                r0 = (g * 4 + tb) * 128
                DMA("sp", xin[:, :], x_d[r0:r0 + 128, :], [], [xin.r()])
                for half in range(2):
                    bk = rot()
                    for c4 in range(4):
                        c = half * 4 + c4
                        TR(bk[:, c4 * 128:(c4 + 1) * 128], xin[:, c * 128:(c + 1) * 128], ident, [xin.r()] + CR, [bk.r()])
                    for c4 in range(4):
                        c = half * 4 + c4
                        CP(xg[:, c, tb * 128:(tb + 1) * 128], bk[:, c4 * 128:(c4 + 1) * 128], [bk.r()], [xg.r(c)],
                           eng=("act" if c4 % 2 else "dve"))

        outregs = []

        def store_x(g):
            for tb in range(4):
                r0 = (g * 4 + tb) * 128
                for half in range(2):
                    bk = rot()
                    for c4 in range(4):
                        c = half * 4 + c4
                        TR(bk[:, c4 * 128:(c4 + 1) * 128], xg[:, c, tb * 128:(tb + 1) * 128], ident, [xg.r(c)] + CR, [bk.r()])
                    CP(xout[:, half * 512:(half + 1) * 512], bk[:, :], [bk.r()], [xout.r()], eng=("act" if half else "dve"))
                DMA("sp", out_d[r0:r0 + 128, :], xout[:, :], [xout.r()], [])

        for g in range(n_groups):
            if PH("x"):
                load_x(g)
            for l in layers:
                if PH("ffn1"):
                    ffn(l, 0)
                if PH("norm"):
                    norm_x(V_MIX + 8 * l)
                mq = mem_prep(l) if PH("mem") else None
                if PH("mixer"):
                    if l < 2:
                        hgrn(l, g)
                    else:
                        fox(l, g)
                if PH("mem"):
                    mem_attn(l, g, mq)
                if PH("wout"):
                    w_out(l)
                if PH("ffn2"):
                    ffn(l, 1)
                if l == 1 and want_kv:
                    kv_stage(g)
            if PH("x"):
                store_x(g)
        k.op("sp", None, [], [xout.r()])
        k.emit(st)
    return nc


_NC_CACHE = {}


def prep_inputs(inp, b):
    import ml_dtypes
    cf, cb = host_consts()
    wa = inp["w_in_a"]
    wb = inp["w_in_b"]
    ia = []
    for hd in range(6):
        for w in range(4):
            ia.extend(range(w * 768 + hd * 128, w * 768 + hd * 128 + 128))
    ia.extend(range(3072, 3328))
    ib = []
    for j in range(6):
        for w in range(2):
            ib.extend(range(w * 768 + j * 128, w * 768 + j * 128 + 128))
    ib.extend(range(1536, 1792))
    return cf, cb, np.ascontiguousarray(wa[:, :, ia]), np.ascontiguousarray(wb[:, :, ib])


def kernel(**inp):
    inp = {k_: np.asarray(v) for k_, v in inp.items()}
    B = inp["x"].shape[0]
    if "nc" not in _NC_CACHE:
        _NC_CACHE["nc"] = build()
    nc = _NC_CACHE["nc"]
    cf, cb, wina, winb = prep_inputs(inp, 0)
    vecs = host_vecs(inp)
    shared = {
        "vecs": vecs, "cstf": cf, "cstb": cb,
        "ffn1_w_gate": inp["ffn1_w_gate"], "ffn2_w_gate": inp["ffn2_w_gate"],
        "ffn1_w_up": inp["ffn1_w_up"], "ffn2_w_up": inp["ffn2_w_up"],
        "ffn1_w_down": inp["ffn1_w_down"], "ffn2_w_down": inp["ffn2_w_down"],
        "w_mem_kv": inp["w_mem_kv"], "wina": wina, "winb": winb, "w_kv": inp["w_kv"], "w_out": inp["w_out"],
    }
    in_maps = []
    for b in range(B):
        m = dict(shared)
        m["x"] = np.ascontiguousarray(inp["x"][b])
        m["mem"] = np.ascontiguousarray(inp["mem"][b])
        in_maps.append(m)
    res = run_bass_kernel_spmd(nc, in_maps, core_ids=list(range(B)))
    return np.stack([np.asarray(r["out"]) for r in res.results], axis=0).astype(np.float32)
```

```python
import contextlib
import numpy as np
import concourse.bass as bass
import concourse.mybir as mybir
from concourse.bass_utils import run_bass_kernel_spmd

F32 = mybir.dt.float32
BF16 = mybir.dt.bfloat16
AF = mybir.ActivationFunctionType
ALU = mybir.AluOpType


class Reg:
    __slots__ = ("w", "r", "excl")

    def __init__(self):
        self.w = []
        self.r = []
        self.excl = False


class Tile:
    def __init__(self, t):
        self.t = t
        self.regs = {}

    def r(self, key=None):
        g = self.regs.get(key)
        if g is None:
            g = self.regs[key] = Reg()
        return g

    def __getitem__(self, idx):
        return self.t[idx]


class Op:
    __slots__ = ("eng", "fn", "deps", "signal", "sigval", "dma", "dsem", "dval", "dprev")


class Sched:
    ENGS = ("pe", "act", "dve", "pool", "sp")

    def __init__(self, nc, ndma=12):
        self.nc = nc
        self.ops = []
        self.byeng = {e: [] for e in self.ENGS}
        self.ndma = ndma
        self.dma_count = {e: 0 for e in self.ENGS}
        self.dma_semcount = {}

    def _record(self, eng, fn, reads, writes, dma):
        i = len(self.ops)
        o = Op()
        o.eng = eng
        o.fn = fn
        o.signal = False
        o.sigval = 0
        o.dma = dma
        o.dsem = None
        o.dval = 0
        o.dprev = 0
        deps = set()
        for r in reads:
            deps.update(r.w)
            if r.excl:
                deps.update(d for d in r.r if self.ops[d].eng != eng)
        for w in writes:
            deps.update(w.w)
            if dma:
                deps.update(w.r)
            else:
                deps.update(d for d in w.r if self.ops[d].dma or self.ops[d].eng != eng)
        for r in reads:
            r.r.append(i)
        for w in writes:
            w.w = [i]
            w.r = []
        red = {}
        dd = []
        for d in deps:
            do = self.ops[d]
            if do.dma:
                dd.append(d)
            else:
                if do.eng == "pe" and eng == "pe" and not dma:
                    continue
                if red.get(do.eng, -1) < d:
                    red[do.eng] = d
        o.deps = dd + list(red.values())
        for d in o.deps:
            self.ops[d].signal = True
        if dma:
            n = self.dma_count[eng]
            self.dma_count[eng] = n + 1
            key = (eng, n % self.ndma)
            c = self.dma_semcount.get(key, 0)
            o.dsem = key
            o.dprev = 16 * c
            o.dval = 16 * (c + 1)
            self.dma_semcount[key] = c + 1
        self.ops.append(o)
        self.byeng[eng].append(i)
        return i

    def op(self, eng, fn, reads=(), writes=()):
        return self._record(eng, fn, reads, writes, False)

    def dma(self, eng, fn, reads=(), writes=()):
        return self._record(eng, fn, reads, writes, True)

    def emit(self, stack):
        nc = self.nc
        esem = {e: stack.enter_context(nc.semaphore("s_" + e)) for e in self.ENGS}
        dsem = {}
        for key in self.dma_semcount:
            dsem[key] = stack.enter_context(nc.semaphore("d_%s%d" % key))
        for e in self.ENGS:
            c = 0
            for i in self.byeng[e]:
                o = self.ops[i]
                if o.signal and not o.dma:
                    c += 1
                    o.sigval = c
        block = stack.enter_context(nc.Block())
        ops = self.ops

        def run(e, h):
            wm = {}
            for i in self.byeng[e]:
                o = ops[i]
                waits = {}
                for d in o.deps:
                    do = ops[d]
                    if do.dma:
                        s, v = dsem[do.dsem], do.dval
                    else:
                        s, v = esem[do.eng], do.sigval
                    k = id(s)
                    if waits.get(k, (0,))[0] < v:
                        waits[k] = (v, s)
                if o.dma and o.dprev > 0:
                    s = dsem[o.dsem]
                    k = id(s)
                    if waits.get(k, (0,))[0] < o.dprev:
                        waits[k] = (o.dprev, s)
                for k, (v, s) in waits.items():
                    if wm.get(k, 0) >= v:
                        continue
                    h.wait_ge(s, v)
                    wm[k] = v
                if o.fn is None:
                    continue
                ins = o.fn(h)
                if o.dma:
                    ins.then_inc(dsem[o.dsem], 16)
                elif o.signal:
                    ins.then_inc(esem[e], 1)

        @block.tensor
        def _(h):
            run("pe", h)

        @block.scalar
        def _(h):
            run("act", h)

        @block.vector
        def _(h):
            run("dve", h)

        @block.gpsimd
        def _(h):
            run("pool", h)

        @block.sync
        def _(h):
            run("sp", h)


S, D, NL, TG, NG = 2048, 1024, 4, 512, 4
DFF = 2816
EPS = 1e-6
NEG = -30000.0

V_FFN1, V_MIX, V_FFN2, V_MEMN, V_KVN = 0, 32, 64, 96, 128
V_OG, V_LB, V_FQG, V_FKG, V_MQG, V_MKG, V_FB = 136, 138, 150, 152, 153, 157, 161
NV = 162
CF_ID, CF_CNEG, CF_CMASK = 0, 128, 256
NCF = 768
CB_ID, CB_TRI, CB_BD, CB_ONES, CB_SEL = 0, 128, 256, 384, 512
NCB = 512 + 12 * 128


def host_consts():
    import ml_dtypes
    cf = np.zeros((128, NCF), np.float32)
    cf[:, CF_ID:CF_ID + 128] = np.eye(128, dtype=np.float32)
    s = np.arange(128)[:, None]
    t = np.arange(128)[None, :]
    cf[:, CF_CNEG:CF_CNEG + 128] = np.where(t >= s, 0.0, NEG)
    cm = np.ones((128, 512), np.float32)
    cm[:, 0::64] = 0.0
    cf[:, CF_CMASK:CF_CMASK + 512] = cm
    cb = np.zeros((128, NCB), np.float32)
    cb[:, CB_ID:CB_ID + 128] = np.eye(128)
    cb[:, CB_TRI:CB_TRI + 128] = ((t >= s) & ((t // 64) == (s // 64)))
    cb[:, CB_BD:CB_BD + 128] = ((t // 64) == (s // 64))
    cb[:, CB_ONES:CB_ONES + 128] = 1.0
    for r in range(3):
        for grp in range(4):
            for i in range(3):
                h = 3 * grp + i
                cb[h, CB_SEL + (r * 4 + grp) * 128 + 32 * i + r] = 1.0
    return cf, cb.astype(ml_dtypes.bfloat16)


def host_vecs(inp):
    v = np.zeros((128, NV), np.float32)

    def fm(a):
        return np.ascontiguousarray(a.reshape(-1, 128).T)

    for l in range(NL):
        v[:, V_FFN1 + 8 * l:V_FFN1 + 8 * l + 8] = fm(inp["ffn1_norm"][l])
        v[:, V_MIX + 8 * l:V_MIX + 8 * l + 8] = fm(inp["mix_norm"][l])
        v[:, V_FFN2 + 8 * l:V_FFN2 + 8 * l + 8] = fm(inp["ffn2_norm"][l])
        v[:, V_MEMN + 8 * l:V_MEMN + 8 * l + 8] = fm(inp["mem_norm"][l])
        v[:, V_MQG + l] = np.tile(inp["mem_q_gain"][l], 2)
        v[:, V_MKG + l] = np.tile(inp["mem_k_gain"][l], 2)
    v[:, V_KVN:V_KVN + 8] = fm(inp["kv_norm"])
    for l in range(2):
        v[:, V_OG + l] = inp["hgrn_o_gain"][l]
        v[:, V_LB + 6 * l:V_LB + 6 * l + 6] = fm(inp["hgrn_lb_logits"][l])
        v[:, V_FQG + l] = np.tile(inp["fox_q_gain"][l], 2)
    v[:, V_FKG] = np.tile(inp["fox_k_gain"], 2)
    v[0:12, V_FB] = inp["fox_f_bias"]
    return v


def build(n_groups=NG, layers=(0, 1, 2, 3), want_kv=True, phases=None, shrink=False):
    NLW = 1 if shrink else NL
    PH = (lambda p: True) if phases is None else (lambda p: p in phases)
    nc = bass.Bass("TRN2", target_bir_lowering=False)

    def din(name, shape, dt=F32):
        return nc.dram_tensor(name, list(shape), dt, kind="ExternalInput").ap()

    x_d = din("x", [S, D])
    mem_d = din("mem", [256, D])
    vecs_d = din("vecs", [128, NV])
    cstf_d = din("cstf", [128, NCF])
    cstb_d = din("cstb", [128, NCB], BF16)
    wg_d = [din("ffn1_w_gate", [NLW, D, DFF]), din("ffn2_w_gate", [NLW, D, DFF])]
    wu_d = [din("ffn1_w_up", [NLW, D, DFF]), din("ffn2_w_up", [NLW, D, DFF])]
    wd_d = [din("ffn1_w_down", [NLW, DFF, D]), din("ffn2_w_down", [NLW, DFF, D])]
    wmem_d = din("w_mem_kv", [NLW, D, 512])
    wina_d = din("wina", [2, D, 3328])
    winb_d = din("winb", [2, D, 1792])
    wkv_d = din("w_kv", [D, 1548])
    wo_d = din("w_out", [NLW, D, D])
    out_d = nc.dram_tensor("out", [S, D], F32, kind="ExternalOutput").ap()

    with contextlib.ExitStack() as st:
        k = Sched(nc)

        def sb(name, shape, dt):
            return Tile(st.enter_context(nc.sbuf_tensor("s_" + name, list(shape), dt)))

        banks = [Tile(st.enter_context(nc.psum_tensor("pb%d" % i, [128, 512], F32))) for i in range(8)]
        for b_ in banks:
            b_.r().excl = True
        rotc = [0]
        rotn = [6]

        def rot():
            b = banks[rotc[0] % rotn[0]]
            rotc[0] += 1
            return b

        NUM, DEN = banks[6], banks[7]

        xg = sb("xg", [128, 8, TG], F32)
        h = sb("h", [128, 8, TG], BF16)
        act = sb("act", [128, 11, TG], BF16)
        mixed = sb("mixed", [128, 8, TG], BF16)
        kT = sb("kT", [128, 6, S], BF16)
        vv = sb("vv", [128, 16, 768], BF16)
        Dh = sb("Dh", [128, 4, S], BF16)
        negD = sb("negD", [128, 16, 12], F32)
        memh = sb("memh", [128, 8, 256], BF16)
        kmT = sb("kmT", [128, NL, 2, 256], BF16)
        vm = sb("vm", [128, NL, 2, 256], BF16)
        vecs = sb("vecs", [128, NV], F32)
        cstf = sb("cstf", [128, NCF], F32)
        cstb = sb("cstb", [128, NCB], BF16)
        der = sb("der", [128, 64], F32)
        fac = sb("fac", [128, 8, 8], F32)
        Sst = sb("Sst", [128, 2, 6, 128], F32)
        carry = sb("carry", [128, 1], F32)
        xin = sb("xin", [128, D], F32)
        xout = xin
        NSLOT = 4
        slots = [sb("w%d" % i, [128, 4096], BF16) for i in range(NSLOT)]
        NF, NB = 10, 10
        Fp = [sb("F%d" % i, [128, 512], F32) for i in range(NF)]
        Bp = [sb("B%d" % i, [128, 512], BF16) for i in range(NB)]
        Sb2 = [sb("Sb%d" % i, [128, 128], BF16) for i in range(8)]
        Tm2 = [sb("Tm%d" % i, [128, 128], F32) for i in range(3)]
        Xb = [sb("Xb%d" % i, [128, 512], BF16) for i in range(4)]
        fc_, bc_, sc_, tc_ = [0], [0], [0], [0]

        def Ft():
            t = Fp[8 + fc_[0] % 2]
            fc_[0] += 1
            return t

        def Bt():
            t = Xb[2 + bc_[0] % 2]
            bc_[0] += 1
            return t

        ptc_ = [0]

        def PTt():
            t = Bp[ptc_[0] % 4]
            ptc_[0] += 1
            return t

        def Tmt():
            t = Tm2[tc_[0] % 3]
            tc_[0] += 1
            return t

        ident = cstf[:, CF_ID:CF_ID + 128]
        cneg = cstf[:, CF_CNEG:CF_CNEG + 128]
        cmask = cstf[:, CF_CMASK:CF_CMASK + 512]
        identb = cstb[:, CB_ID:CB_ID + 128]
        trimask = cstb[:, CB_TRI:CB_TRI + 128]
        bd64 = cstb[:, CB_BD:CB_BD + 128]
        onesb = cstb[:, CB_ONES:CB_ONES + 128]
        CR = [cstf.r(), cstb.r(), vecs.r(), der.r()]

        def MM(out, lhsT, rhs, start, stop, R, W, skip=False):
            if skip:
                k.op("pe", lambda e: e.matmul(out, lhsT, rhs, start=start, stop=stop, skip_group_check=True), R, W)
            else:
                k.op("pe", lambda e: e.matmul(out, lhsT, rhs, start=start, stop=stop), R, W)

        def TR(out, in_, idn, R, W):
            k.op("pe", lambda e: e.transpose(out, in_, idn), R, W)

        def ACT(out, in_, func, R, W, bias=None, scale=None):
            kw = {}
            if bias is not None:
                kw["bias"] = bias
            if scale is not None:
                kw["scale"] = scale
            k.op("act", lambda e: e.activation(out, in_, func, **kw), R, W)

        def TT(out, in0, in1, op, R, W, eng="dve"):
            k.op(eng, lambda e: e.tensor_tensor(out, in0, in1, op), R, W)

        def TS(out, in0, s1, s2, op0, op1, R, W, eng="dve"):
            if s2 is None:
                k.op(eng, lambda e: e.tensor_scalar(out, in0, s1, None, op0), R, W)
            else:
                k.op(eng, lambda e: e.tensor_scalar(out, in0, s1, s2, op0, op1), R, W)

        def STT(out, in0, sc, in1, op0, op1, R, W):
            k.op("dve", lambda e: e.scalar_tensor_tensor(out, in0, sc, in1, op0, op1), R, W)

        def CP(out, in_, R, W, eng="dve"):
            if eng == "act":
                k.op("act", lambda e: e.activation(out, in_, AF.Copy), R, W)
            else:
                k.op(eng, lambda e: e.tensor_copy(out, in_), R, W)

        def RECIP(out, in_, R, W):
            k.op("dve", lambda e: e.reciprocal(out, in_), R, W)

        def DMA(eng, out, in_, R, W):
            k.dma(eng, lambda e: e.dma_start(out=out, in_=in_), R, W)

        slotc = [0]

        def loadw(src, shape):
            t = slots[slotc[0] % NSLOT]
            slotc[0] += 1
            a, b = shape
            view = t[:, 0:a * b].rearrange("p (a b) -> p a b", a=a)
            DMA("pool", view, src, [], [t.r()])
            return view, t

        def wrows(w2d):
            return w2d.rearrange("(k p) n -> p k n", p=128)

        def rsqrt_from(ps_ap, scale, R_ps, W_t, out_ap):
            ACT(out_ap, ps_ap, AF.Ln, R_ps, W_t, bias=EPS, scale=scale)
            ACT(out_ap, out_ap, AF.Exp, W_t, W_t, scale=-0.5)

        DMA("sp", cstf[:, :], cstf_d, [], [cstf.r()])
        DMA("sp", cstb[:, :], cstb_d, [], [cstb.r()])
        DMA("sp", vecs[:, :], vecs_d, [], [vecs.r()])
        if PH("der"):
            k.op("dve", lambda e: e.memset(der[:, 0:6], 0.0), [], [der.r()])
            k.op("dve", lambda e: e.memset(der[:, 12:18], 1.0), [der.r()], [der.r()])
            TT(der[:, 6:12], vecs[:, V_LB + 6:V_LB + 12], vecs[:, V_LB:V_LB + 6], ALU.subtract, [vecs.r(), der.r()], [der.r()])
            ACT(der[:, 6:12], der[:, 6:12], AF.Sigmoid, [der.r()], [der.r()])
            TS(der[:, 18:24], der[:, 6:12], -1.0, 1.0, ALU.mult, ALU.add, [der.r()], [der.r()])
            TS(der[:, 24:26], vecs[:, V_FQG:V_FQG + 2], 0.125, None, ALU.mult, None, [vecs.r(), der.r()], [der.r()])
            TS(der[:, 26:30], vecs[:, V_MQG:V_MQG + 4], 0.125, None, ALU.mult, None, [vecs.r(), der.r()], [der.r()])
            TS(der[:, 30:31], vecs[:, V_FB:V_FB + 1], -1.0, None, ALU.mult, None, [vecs.r(), der.r()], [der.r()])
            TS(der[:, 32:44], der[:, 12:24], 0.5, None, ALU.mult, None, [der.r()], [der.r()])
            TT(der[:, 44:56], der[:, 32:44], der[:, 0:12], ALU.add, [der.r()], [der.r()])
            k.op("dve", lambda e: e.memset(Sst[:, :, :, :], 0.0), [], [Sst.r((l, hd)) for l in range(2) for hd in range(6)])
            k.op("dve", lambda e: e.memset(carry[:, :], 0.0), [], [carry.r()])

        def LB(l, hd):
            return der[:, 6 * l + hd:6 * l + hd + 1]

        def OML(l, hd):
            return der[:, 12 + 6 * l + hd:12 + 6 * l + hd + 1]

        def HOML(l, hd):
            return der[:, 32 + 6 * l + hd:32 + 6 * l + hd + 1]

        def HB(l, hd):
            return der[:, 44 + 6 * l + hd:44 + 6 * l + hd + 1]

        memT = [Fp[0], Fp[1], Fp[2], Fp[3]]

        def memT_ap(c):
            return memT[c // 2][:, (c % 2) * 256:(c % 2) * 256 + 256]

        for tb in (range(2) if PH("setup_mem") else ()):
            DMA("sp", xin[:, :], mem_d[tb * 128:(tb + 1) * 128, :], [], [xin.r()])
            for half in range(2):
                bk = rot()
                for c4 in range(4):
                    c = half * 4 + c4
                    TR(bk[:, c4 * 128:(c4 + 1) * 128], xin[:, c * 128:(c + 1) * 128], ident, [xin.r()] + CR, [bk.r()])
                for c4 in range(4):
                    c = half * 4 + c4
                    CP(memT_ap(c)[:, tb * 128:(tb + 1) * 128], bk[:, c4 * 128:(c4 + 1) * 128], [bk.r()], [memT[c // 2].r()])
        nsb = rot()
        for c in (range(8) if PH("setup_mem") else ()):
            sq = Bt()
            ACT(sq[:, 0:256], memT_ap(c), AF.Square, [memT[c // 2].r()], [sq.r()])
            MM(nsb[:, 0:256], onesb, sq[:, 0:256], c == 0, c == 7, [sq.r()] + CR, [nsb.r()])
        rsm = Fp[4]
        if PH("setup_mem"):
            rsqrt_from(nsb[:, 0:256], 1.0 / D, [nsb.r()], [rsm.r()], rsm[:, 0:256])
        for c in (range(8) if PH("setup_mem") else ()):
            TT(memh[:, c, :], memT_ap(c), rsm[:, 0:256], ALU.mult, [memT[c // 2].r(), rsm.r()], [memh.r()])
        memn = act
        for l in (range(NL) if PH("setup_mem") else ()):
            for c in range(8):
                TS(memn[:, c, 0:256], memh[:, c, :], vecs[:, V_MEMN + 8 * l + c:V_MEMN + 8 * l + c + 1], None, ALU.mult, None,
                   [memh.r()] + CR, [act.r()])
            wm, wmt = loadw(wrows(wmem_d[min(l, NLW - 1)]), (8, 512))
            for c2 in range(2):
                kp = rot()
                for c in range(8):
                    MM(kp[:, 0:256], wm[:, c, c2 * 128:(c2 + 1) * 128], memn[:, c, 0:256], c == 0, c == 7, [wmt.r(), act.r()], [kp.r()])
                sq = Bt()
                ACT(sq[:, 0:256], kp[:, 0:256], AF.Square, [kp.r()], [sq.r()])
                ns = rot()
                MM(ns[:, 0:256], bd64, sq[:, 0:256], True, True, [sq.r()] + CR, [ns.r()])
                rs = Ft()
                rsqrt_from(ns[:, 0:256], 1.0 / 64, [ns.r()], [rs.r()], rs[:, 0:256])
                STT(kmT[:, l, c2, :], kp[:, 0:256], vecs[:, V_MKG + l:V_MKG + l + 1], rs[:, 0:256], ALU.mult, ALU.mult,
                    [kp.r(), rs.r()] + CR, [kmT.r(l)])
            for tb in range(2):
                vp = rot()
                for c in range(8):
                    MM(vp[:, 0:256], memn[:, c, tb * 128:(tb + 1) * 128], wm[:, c, 256:512], c == 0, c == 7, [wmt.r(), act.r()], [vp.r()])
                CP(vm[:, l, tb, :], vp[:, 0:256], [vp.r()], [vm.r(l)])

        XR = [xg.r(c) for c in range(8)]

        def norm_x(gcol):
            ns = rot()
            for c in range(8):
                sq = Bt()
                ACT(sq[:, :], xg[:, c, :], AF.Square, [xg.r(c)], [sq.r()])
                MM(ns[:, :], onesb, sq[:, :], c == 0, c == 7, [sq.r()] + CR, [ns.r()])
            rs = Ft()
            rsqrt_from(ns[:, :], 1.0 / D, [ns.r()], [rs.r()], rs[:, :])
            for c in range(8):
                eng = "dve"
                k.op(eng, lambda e, c=c: e.scalar_tensor_tensor(h[:, c, :], xg[:, c, :], vecs[:, gcol + c:gcol + c + 1], rs[:, :],
                                                                ALU.mult, ALU.mult),
                     [xg.r(c), rs.r()] + CR, [h.r(c)])

        def ffn(l, which):
            norm_x((V_FFN1 if which == 0 else V_FFN2) + 8 * l)
            lw = min(l, NLW - 1)
            Wg, Wu, Wd = wrows(wg_d[which][lw]), wrows(wu_d[which][lw]), wd_d[which][lw].rearrange("(c p) d -> p c d", p=128)
            for half in range(2):
                f0 = half * 11
                fi = 0
                for nch in (4, 4, 3):
                    c0 = (f0 + fi) * 128
                    g_v, g_t = loadw(Wg[:, :, c0:c0 + nch * 128], (8, nch * 128))
                    u_v, u_t = loadw(Wu[:, :, c0:c0 + nch * 128], (8, nch * 128))
                    for j in range(nch):
                        G, U = rot(), rot()
                        for c in range(8):
                            MM(G[:, :], g_v[:, c, j * 128:(j + 1) * 128], h[:, c, :], c == 0, c == 7, [g_t.r(), h.r(c)], [G.r()])
                        for c in range(8):
                            MM(U[:, :], u_v[:, c, j * 128:(j + 1) * 128], h[:, c, :], c == 0, c == 7, [u_t.r(), h.r(c)], [U.r()])
                        sg = Ft()
                        ACT(sg[:, :], G[:, :], AF.Silu, [G.r()], [sg.r()])
                        TT(act[:, fi, :], sg[:, :], U[:, :], ALU.mult, [sg.r(), U.r()], [act.r()])
                        fi += 1
                for db in range(4):
                    d_v, d_t = loadw(Wd[:, f0:f0 + 11, db * 256:(db + 1) * 256], (11, 256))
                    for dc in range(2):
                        d = db * 2 + dc
                        Y = rot()
                        for f in range(11):
                            MM(Y[:, :], d_v[:, f, dc * 128:(dc + 1) * 128], act[:, f, :], f == 0, f == 10, [d_t.r(), act.r()], [Y.r()])
                        STT(xg[:, d, :], Y[:, :], 0.5, xg[:, d, :], ALU.mult, ALU.add, [Y.r(), xg.r(d)], [xg.r(d)])

        def attn_core(q, qreg, kT_ap, nkb, v_ap, vreg, kreg, out_ap, outreg, gate=None, gatereg=None,
                      aug=None, bias=None, causal_g=None, acc=None):
            NUM, DEN = acc if acc is not None else (banks[6], banks[7])
            def stageA(kb):
                res = []
                diag = causal_g is not None and kb >= 4 * causal_g
                c0 = (kb - 4 * causal_g) * 128 if diag else 0
                Ss = [rot(), rot()]
                for e in range(2):
                    rows = slice(64 * e, 64 * e + 64)
                    MM(Ss[e][:, c0:512], kT_ap(e)[:, kb * 128:(kb + 1) * 128], q[rows, c0:512], True, aug is None,
                       [kreg, qreg], [Ss[e].r()])
                if aug is not None:
                    for e in range(2):
                        o1, d1 = aug(e, c0)
                        MM(Ss[e][:, c0:512], o1, d1, False, True, [Dh.r()] + CR, [Ss[e].r()])
                for e in range(2):
                    Sb_ = Ss[e]
                    PT = PTt()
                    bi = bias(kb, e) if bias is not None else None
                    br = [negD.r()] if bias is not None else []
                    if diag:
                        tm = Tmt()
                        TT(tm[:, :], Sb_[:, c0:c0 + 128], cneg, ALU.add, [Sb_.r()] + CR, [tm.r()])
                        ACT(PT[:, c0:c0 + 128], tm[:, :], AF.Exp, [tm.r()] + br, [PT.r()], bias=bi)
                        if c0 + 128 < 512:
                            ACT(PT[:, c0 + 128:512], Sb_[:, c0 + 128:512], AF.Exp, [Sb_.r()] + br, [PT.r()], bias=bi)
                    else:
                        ACT(PT[:, c0:512], Sb_[:, c0:512], AF.Exp, [Sb_.r()] + br, [PT.r()], bias=bi)
                    res.append((PT, c0))
                return res

            def stageB(kb, res):
                for e in range(2):
                    PT, c0 = res[e]
                    rows = slice(64 * e, 64 * e + 64)
                    MM(NUM[rows, c0:512], v_ap(kb, e), PT[:, c0:512], kb == 0, kb == nkb - 1, [PT.r(), vreg], [NUM.r()])
                for e in range(2):
                    PT, c0 = res[e]
                    rows = slice(64 * e, 64 * e + 64)
                    MM(DEN[rows, c0:512], onesb[:, 0:64], PT[:, c0:512], kb == 0, kb == nkb - 1, [PT.r()] + CR, [DEN.r()])

            prev = stageA(0)
            for kb in range(nkb):
                nxt = stageA(kb + 1) if kb + 1 < nkb else None
                stageB(kb, prev)
                prev = nxt
            rec = Ft()
            if gate is not None:
                STT(rec[:, :], gate, 1.0, DEN[:, :], ALU.add, ALU.mult, [gatereg, DEN.r()], [rec.r()])
                RECIP(rec[:, :], rec[:, :], [rec.r()], [rec.r()])
            else:
                RECIP(rec[:, :], DEN[:, :], [DEN.r()], [rec.r()])
            TT(out_ap, NUM[:, :], rec[:, :], ALU.mult, [NUM.r(), rec.r()], [outreg])

        def headnorm_q(P, wq_scalar, width, dst=None):
            sq = Bt()
            ACT(sq[:, :], P[:, :], AF.Square, [P.r()], [sq.r()])
            ns = rot()
            MM(ns[:, :], bd64, sq[:, :], True, True, [sq.r()] + CR, [ns.r()])
            rs = Ft()
            rsqrt_from(ns[:, :], 1.0 / 64, [ns.r()], [rs.r()], rs[:, :])
            qn = dst if dst is not None else Bp[4]
            STT(qn[:, :], P[:, :], wq_scalar, rs[:, :], ALU.mult, ALU.mult, [P.r(), rs.r()] + CR, [qn.r()])
            return qn

        class _ActView:
            def __init__(self, i):
                self.ap = act[:, i, :]

            def r(self, key=None):
                return act.r()

            def __getitem__(self, idx):
                return self.ap[idx]

        def mem_prep(l):
            if l < 2:
                src = wrows(wina_d[l])[:, :, 3072:3328]
            else:
                src = wrows(winb_d[l - 2])[:, :, 1536:1792]
            wq, wqt = loadw(src, (8, 256))
            qns = []
            for c2 in range(2):
                QM = rot()
                for c in range(8):
                    MM(QM[:, :], wq[:, c, c2 * 128:(c2 + 1) * 128], h[:, c, :], c == 0, c == 7, [wqt.r(), h.r(c)], [QM.r()])
                qns.append(headnorm_q(QM, der[:, 26 + l:27 + l], 64, dst=_ActView(c2)))
            return qns

        def mem_attn(l, g, qns):
            rotn[0] = 4
            accs = ((banks[4], banks[5]), (banks[6], banks[7]))
            for c2 in range(2):
                qn = qns[c2]
                attn_core(qn[:, :], qn.r(),
                          lambda e, c2=c2: kmT[64 * e:64 * e + 64, l, c2, :], 2,
                          lambda kb, e, c2=c2: vm[:, l, kb, (c2 * 2 + e) * 64:(c2 * 2 + e) * 64 + 64], vm.r(l), kmT.r(l),
                          mixed[:, 6 + c2, :], mixed.r(6 + c2), acc=accs[c2])
            rotn[0] = 6

        def w_out(l):
            Wo = wrows(wo_d[min(l, NLW - 1)])
            for hb in range(2):
                o_v, o_t = loadw(Wo[:, :, hb * 512:(hb + 1) * 512], (8, 512))
                for dd in range(4):
                    d = hb * 4 + dd
                    Y = rot()
                    for c in range(8):
                        MM(Y[:, :], o_v[:, c, dd * 128:(dd + 1) * 128], mixed[:, c, :], c == 0, c == 7, [o_t.r(), mixed.r(c)], [Y.r()])
                    TT(xg[:, d, :], Y[:, :], xg[:, d, :], ALU.add, [Y.r(), xg.r(d)], [xg.r(d)])

        def hgrn(l, g):
            Wa = wrows(wina_d[l])
            th, logf, cc, dq, eq, ek, qs, t1 = Fp[0:8]
            ktT, scT = Xb[0], Xb[1]

            B_Q, B_F, B_G, B_V, B_KA, B_KB, B_O, B_T = banks

            def S1p(hd):
                par = hd % 2
                wA, wAt = loadw(Wa[:, :, hd * 512:(hd + 1) * 512], (8, 512))
                Q, Fq, Gq, Vq = B_Q, B_F, B_G, B_V
                for c in range(8):
                    MM(Q[:, :], wA[:, c, 0:128], h[:, c, :], c == 0, c == 7, [wAt.r(), h.r(c)], [Q.r()])
                for c in range(8):
                    MM(Gq[:, :], wA[:, c, 384:512], h[:, c, :], c == 0, c == 7, [wAt.r(), h.r(c)], [Gq.r()])
                for c in range(8):
                    MM(Fq[:, :], wA[:, c, 128:256], h[:, c, :], c == 0, c == 7, [wAt.r(), h.r(c)], [Fq.r()])
                for tb in range(4):
                    for c in range(8):
                        MM(Vq[:, tb * 128:(tb + 1) * 128], h[:, c, tb * 128:(tb + 1) * 128], wA[:, c, 256:384], c == 0, c == 7,
                           [wAt.r(), h.r(c)], [Vq.r()])

            def S1f(hd):
                par = hd % 2
                qt, kt, gs, vtok, kt2 = Bp[5 * par:5 * par + 5]
                Q, Fq, Gq, Vq = B_Q, B_F, B_G, B_V
                ACT(qs[:, :], Q[:, :], AF.Silu, [Q.r()], [qs.r()])
                ACT(gs[:, :], Gq[:, :], AF.Silu, [Gq.r()], [gs.r()])
                ACT(th[:, :], Fq[:, :], AF.Tanh, [Fq.r()], [th.r()], scale=0.5)
                CP(vtok[:, :], Vq[:, :], [Vq.r()], [vtok.r()], eng="act")

            def S1e(hd):
                par = hd % 2
                qt, kt, gs, vtok, kt2 = Bp[5 * par:5 * par + 5]
                fo = 4 * par
                freg = fac.r(par)
                ACT(logf[:, :], th[:, :], AF.Ln, [th.r()] + CR, [logf.r()], bias=HB(l, hd), scale=HOML(l, hd))
                k.op("dve", lambda e: e.tensor_tensor_scan(cc[:, :], cmask, logf[:, :], 0.0, ALU.mult, ALU.add),
                     [logf.r()] + CR, [cc.r()])
                TT(fac[:, fo + 3, :], cc[:, 63:512:64], cc[:, 31:512:64], ALU.subtract, [cc.r()], [freg])
                cm_b = cc[:, 31:512:64].unsqueeze(2).to_broadcast([128, 8, 64])
                TT(dq[:, :].rearrange("p (a b) -> p a b", b=64), cc[:, :].rearrange("p (a b) -> p a b", b=64), cm_b, ALU.subtract,
                   [cc.r()], [dq.r()])
                ACT(fac[:, fo + 1, :], fac[:, fo + 3, :], AF.Exp, [freg], [freg])
                ACT(ek[:, :], dq[:, :], AF.Exp, [dq.r()], [ek.r()], scale=-1.0)
                ACT(eq[:, :], dq[:, :], AF.Exp, [dq.r()], [eq.r()])
                ACT(fac[:, fo + 0, :], cc[:, 63:512:64], AF.Exp, [cc.r(), freg], [freg])
                ACT(fac[:, fo + 2, :], cc[:, 31:512:64], AF.Exp, [cc.r(), freg], [freg])
                TS(t1[:, :], th[:, :], -1.0, 1.0, ALU.mult, ALU.add, [th.r()], [t1.r()])
                STT(kt[:, :], t1[:, :], HOML(l, hd), ek[:, :], ALU.mult, ALU.mult, [t1.r(), ek.r()] + CR, [kt.r()])
                TT(kt2[:, :].rearrange("p (a b) -> p a b", b=64), kt[:, :].rearrange("p (a b) -> p a b", b=64),
                   fac[:, fo + 1, :].unsqueeze(2).to_broadcast([128, 8, 64]), ALU.mult, [kt.r(), freg], [kt2.r()])
                TT(qt[:, :], qs[:, :], eq[:, :], ALU.mult, [qs.r(), eq.r()], [qt.r()])
                return qt, kt, kt2, gs, vtok, fo, freg

            def S2a(hd, ctx):
                qt, kt, kt2, gs, vtok, fo, freg = ctx
                tp = B_T
                tpb = tp[:, 0:256].bitcast(BF16)
                for tb in range(4):
                    TR(tpb[:, tb * 128:(tb + 1) * 128], kt2[:, tb * 128:(tb + 1) * 128], identb, [kt2.r()] + CR, [tp.r()])
                CP(ktT[:, :], tpb, [tp.r()], [ktT.r()], eng="act")
                KVb = [B_KA, B_KB]
                for n in range(8):
                    b = n // 2
                    rows = slice(64 * (n % 2), 64 * (n % 2) + 64)
                    MM(KVb[n % 2][:, b * 128:(b + 1) * 128], ktT[rows, b * 128:(b + 1) * 128], vtok[rows, b * 128:(b + 1) * 128],
                       True, True, [ktT.r(), vtok.r()], [KVb[n % 2].r()])
                SC = B_T
                for b in range(4):
                    MM(SC[:, b * 128:(b + 1) * 128], kt[:, b * 128:(b + 1) * 128], qt[:, b * 128:(b + 1) * 128], True, True,
                       [kt.r(), qt.r()], [SC.r()])
                TT(scT[:, :].rearrange("p (a b) -> p a b", b=128), SC[:, :].rearrange("p (a b) -> p a b", b=128),
                   trimask.unsqueeze(1).to_broadcast([128, 4, 128]), ALU.mult, [SC.r()] + CR, [scT.r()])
                O = B_O
                for b in range(4):
                    MM(O[:, b * 128:(b + 1) * 128], vtok[:, b * 128:(b + 1) * 128], scT[:, b * 128:(b + 1) * 128], b == 0, False,
                       [vtok.r(), scT.r()], [O.r()], skip=True)

            def S2c(hd, ctx):
                qt, kt, kt2, gs, vtok, fo, freg = ctx
                KVb = [B_KA, B_KB]
                O = B_O
                sreg = Sst.r((l, hd))
                Sap = Sst[:, l, hd, :]
                sbs = []
                for n in range(8):
                    sbt = Sb2[n]
                    TS(sbt[:, :], Sap, fac[:, fo + 2, n:n + 1], None, ALU.mult, None, [sreg, freg], [sbt.r()])
                    STT(Sap, Sap, fac[:, fo + 0, n:n + 1], KVb[n % 2][:, (n // 2) * 128:(n // 2 + 1) * 128], ALU.mult, ALU.add,
                        [sreg, freg, KVb[n % 2].r()], [sreg])
                    sbs.append(sbt)
                for n in range(8):
                    MM(O[:, n * 64:(n + 1) * 64], sbs[n][:, :], qt[:, n * 64:(n + 1) * 64], False, n == 7, [sbs[n].r(), qt.r()], [O.r()], skip=True)

            def S2t(hd, ctx):
                qt, kt, kt2, gs, vtok, fo, freg = ctx
                O = B_O
                sqo = Bt()
                ACT(sqo[:, :], O[:, :], AF.Square, [O.r()], [sqo.r()])
                ns = B_T
                MM(ns[:, :], onesb, sqo[:, :], True, True, [sqo.r()] + CR, [ns.r()])
                rs = Ft()
                rsqrt_from(ns[:, :], 1.0 / 128, [ns.r()], [rs.r()], rs[:, :])
                on = Ft()
                STT(on[:, :], O[:, :], vecs[:, V_OG + l:V_OG + l + 1], rs[:, :], ALU.mult, ALU.mult, [O.r(), rs.r()] + CR, [on.r()])
                TT(mixed[:, hd, :], on[:, :], gs[:, :], ALU.mult, [on.r(), gs.r()], [mixed.r(hd)])

            S1p(0)
            S1f(0)
            S1p(1)
            pending = S1e(0)
            for hd in range(6):
                cur = pending
                S2a(hd, cur)
                if hd + 1 < 6:
                    S1f(hd + 1)
                if hd + 2 < 6:
                    S1p(hd + 2)
                S2c(hd, cur)
                if hd + 1 < 6:
                    pending = S1e(hd + 1)
                S2t(hd, cur)

        def fox(l, g):
            j2 = l - 2
            Wb = wrows(winb_d[j2])
            rotn[0] = 4
            accs = ((banks[4], banks[5]), (banks[6], banks[7]))

            def prep(j):
                wB, wBt = loadw(Wb[:, :, j * 256:(j + 1) * 256], (8, 256))
                Qp, Gp = rot(), rot()
                for c in range(8):
                    MM(Qp[:, :], wB[:, c, 0:128], h[:, c, :], c == 0, c == 7, [wBt.r(), h.r(c)], [Qp.r()])
                for c in range(8):
                    MM(Gp[:, :], wB[:, c, 128:256], h[:, c, :], c == 0, c == 7, [wBt.r(), h.r(c)], [Gp.r()])
                qn = headnorm_q(Qp, der[:, 24 + j2:25 + j2], 64, dst=Bp[4 + j % 2])
                gate = Fp[6 + j % 2]
                ACT(gate[:, :], Gp[:, :], AF.Exp, [Gp.r()], [gate.r()], scale=-1.0)
                return qn, gate

            pending = prep(0)
            for j in range(6):
                qn, gate = pending
                if j + 1 < 6:
                    pending = prep(j + 1)

                def aug(e, c0, j=j):
                    hh = 2 * j + e
                    base = 32 * (hh % 3)
                    grp = hh // 3
                    return onesb[base:base + 32, :], Dh[base:base + 32, grp, g * 512 + c0:(g + 1) * 512]

                attn_core(qn[:, :], qn.r(),
                          lambda e, j=j: kT[64 * e:64 * e + 64, j, :], 4 * g + 4,
                          lambda kb, e, j=j: vv[:, kb, (2 * j + e) * 64:(2 * j + e) * 64 + 64], vv.r(), kT.r(),
                          mixed[:, j, :], mixed.r(j), gate=gate[:, :], gatereg=gate.r(),
                          aug=aug, bias=lambda kb, e, j=j: negD[:, kb, 2 * j + e:2 * j + e + 1], causal_g=g, acc=accs[j % 2])
            rotn[0] = 6

        def kv_stage(g):
            norm_x(V_KVN)
            Wk = wrows(wkv_d)
            kw = []
            for (c0, n) in ((0, 4), (4, 2)):
                kw.append(loadw(Wk[:, :, c0 * 128:(c0 + n) * 128], (8, n * 128)))

            def kproj(j):
                k_v, k_t = kw[0] if j < 4 else kw[1]
                jj = j if j < 4 else j - 4
                Kp = rot()
                for c in range(8):
                    MM(Kp[:, :], k_v[:, c, jj * 128:(jj + 1) * 128], h[:, c, :], c == 0, c == 7, [k_t.r(), h.r(c)], [Kp.r()])
                return Kp

            def kfin(j, Kp):
                sq = Bt()
                ACT(sq[:, :], Kp[:, :], AF.Square, [Kp.r()], [sq.r()])
                ns = rot()
                MM(ns[:, :], bd64, sq[:, :], True, True, [sq.r()] + CR, [ns.r()])
                rs = Ft()
                rsqrt_from(ns[:, :], 1.0 / 64, [ns.r()], [rs.r()], rs[:, :])
                STT(kT[:, j, g * 512:(g + 1) * 512], Kp[:, :], vecs[:, V_FKG:V_FKG + 1], rs[:, :], ALU.mult, ALU.mult,
                    [Kp.r(), rs.r()] + CR, [kT.r()])

            prevK = kproj(0)
            for j in range(6):
                nxtK = kproj(j + 1) if j + 1 < 6 else None
                kfin(j, prevK)
                prevK = nxtK
            for (c0, n) in ((0, 512), (512, 256)):
                v_v, v_t = loadw(Wk[:, :, 768 + c0:768 + c0 + n], (8, n))
                for tb in range(4):
                    Vp = rot()
                    for c in range(8):
                        MM(Vp[:, 0:n], h[:, c, tb * 128:(tb + 1) * 128], v_v[:, c, :], c == 0, c == 7, [v_t.r(), h.r(c)], [Vp.r()])
                    CP(vv[:, 4 * g + tb, c0:c0 + n], Vp[:, 0:n], [Vp.r()], [vv.r()], eng=("act" if tb % 2 else "dve"))
            f_v, f_t = loadw(Wk[:, :, 1536:1548], (8, 12))
            FL = rot()
            for c in range(8):
                MM(FL[0:12, :], f_v[:, c, :], h[:, c, :], c == 0, c == 7, [f_t.r(), h.r(c)], [FL.r()])
            ee, Dg, r1, r2 = Fp[0:4]
            ACT(ee[0:12, :], FL[0:12, :], AF.Exp, [FL.r()] + CR, [ee.r()], bias=der[0:12, 30:31], scale=-1.0)
            ACT(ee[0:12, :], ee[0:12, :], AF.Ln, [ee.r()], [ee.r()], bias=1.0)
            for q4 in range(4):
                ini = carry[0:12, 0:1] if q4 == 0 else Dg[0:12, q4 * 128 - 1:q4 * 128]
                k.op("dve", lambda e, q4=q4, ini=ini: e.tensor_tensor_scan(Dg[0:12, q4 * 128:(q4 + 1) * 128], onesb[0:12, :],
                                                                           ee[0:12, q4 * 128:(q4 + 1) * 128], ini, ALU.mult, ALU.subtract),
                     [ee.r(), carry.r(), Dg.r()] + CR, [Dg.r()])
            CP(carry[0:12, 0:1], Dg[0:12, 511:512], [Dg.r()], [carry.r()])
            for tb in range(4):
                tp = rot()
                TR(tp[:, 0:12], Dg[0:12, tb * 128:(tb + 1) * 128], ident[0:12, 0:12], [Dg.r()] + CR, [tp.r()])
                TS(negD[:, 4 * g + tb, :], tp[:, 0:12], -1.0, None, ALU.mult, None, [tp.r()], [negD.r()])
            hi, mid, lo = Bp[0:3]
            CP(hi[0:12, :], Dg[0:12, :], [Dg.r()], [hi.r()])
            TT(r1[0:12, :], Dg[0:12, :], hi[0:12, :], ALU.subtract, [Dg.r(), hi.r()], [r1.r()])
            CP(mid[0:12, :], r1[0:12, :], [r1.r()], [mid.r()])
            TT(r2[0:12, :], r1[0:12, :], mid[0:12, :], ALU.subtract, [r1.r(), mid.r()], [r2.r()])
            CP(lo[0:12, :], r2[0:12, :], [r2.r()], [lo.r()])
            parts = (hi, mid, lo)
            for grp in range(4):
                DP = rot()
                for r in range(3):
                    cs = CB_SEL + (r * 4 + grp) * 128
                    MM(DP[:, :], cstb[0:12, cs:cs + 128], parts[r][0:12, :], r == 0, r == 2, [parts[r].r()] + CR, [DP.r()])
                CP(Dh[:, grp, g * 512:(g + 1) * 512], DP[:, :], [DP.r()], [Dh.r()], eng=("act" if grp % 2 else "dve"))

        def load_x(g):
            for tb in range(4):
                r0 = (g * 4 + tb) * 128
                DMA("sp", xin[:, :], x_d[r0:r0 + 128, :], [], [xin.r()])
                for half in range(2):
                    bk = rot()
                    for c4 in range(4):
                        c = half * 4 + c4
                        TR(bk[:, c4 * 128:(c4 + 1) * 128], xin[:, c * 128:(c + 1) * 128], ident, [xin.r()] + CR, [bk.r()])
                    for c4 in range(4):
                        c = half * 4 + c4
                        CP(xg[:, c, tb * 128:(tb + 1) * 128], bk[:, c4 * 128:(c4 + 1) * 128], [bk.r()], [xg.r(c)],
                           eng=("act" if c4 % 2 else "dve"))

        outregs = []

        def store_x(g):
            for tb in range(4):
                r0 = (g * 4 + tb) * 128
                for half in range(2):
                    bk = rot()
                    for c4 in range(4):
                        c = half * 4 + c4
                        TR(bk[:, c4 * 128:(c4 + 1) * 128], xg[:, c, tb * 128:(tb + 1) * 128], ident, [xg.r(c)] + CR, [bk.r()])
                    CP(xout[:, half * 512:(half + 1) * 512], bk[:, :], [bk.r()], [xout.r()], eng=("act" if half else "dve"))
                DMA("sp", out_d[r0:r0 + 128, :], xout[:, :], [xout.r()], [])

        for g in range(n_groups):
            if PH("x"):
                load_x(g)
            for l in layers:
                if PH("ffn1"):
                    ffn(l, 0)
                if PH("norm"):
                    norm_x(V_MIX + 8 * l)
                mq = mem_prep(l) if PH("mem") else None
                if PH("mixer"):
                    if l < 2:
                        hgrn(l, g)
                    else:
                        fox(l, g)
                if PH("mem"):
                    mem_attn(l, g, mq)
                if PH("wout"):
                    w_out(l)
                if PH("ffn2"):
                    ffn(l, 1)
                if l == 1 and want_kv:
                    kv_stage(g)
            if PH("x"):
                store_x(g)
        k.op("sp", None, [], [xout.r()])
        k.emit(st)
    return nc


_NC_CACHE = {}


def prep_inputs(inp, b):
    import ml_dtypes
    cf, cb = host_consts()
    wa = inp["w_in_a"]
    wb = inp["w_in_b"]
    ia = []
    for hd in range(6):
        for w in range(4):
            ia.extend(range(w * 768 + hd * 128, w * 768 + hd * 128 + 128))
    ia.extend(range(3072, 3328))
    ib = []
    for j in range(6):
        for w in range(2):
            ib.extend(range(w * 768 + j * 128, w * 768 + j * 128 + 128))
    ib.extend(range(1536, 1792))
    return cf, cb, np.ascontiguousarray(wa[:, :, ia]), np.ascontiguousarray(wb[:, :, ib])


def kernel(**inp):
    inp = {k_: np.asarray(v) for k_, v in inp.items()}
    B = inp["x"].shape[0]
    if "nc" not in _NC_CACHE:
        _NC_CACHE["nc"] = build()
    nc = _NC_CACHE["nc"]
    cf, cb, wina, winb = prep_inputs(inp, 0)
    vecs = host_vecs(inp)
    shared = {
        "vecs": vecs, "cstf": cf, "cstb": cb,
        "ffn1_w_gate": inp["ffn1_w_gate"], "ffn2_w_gate": inp["ffn2_w_gate"],
        "ffn1_w_up": inp["ffn1_w_up"], "ffn2_w_up": inp["ffn2_w_up"],
        "ffn1_w_down": inp["ffn1_w_down"], "ffn2_w_down": inp["ffn2_w_down"],
        "w_mem_kv": inp["w_mem_kv"], "wina": wina, "winb": winb, "w_kv": inp["w_kv"], "w_out": inp["w_out"],
    }
    in_maps = []
    for b in range(B):
        m = dict(shared)
        m["x"] = np.ascontiguousarray(inp["x"][b])
        m["mem"] = np.ascontiguousarray(inp["mem"][b])
        in_maps.append(m)
    res = run_bass_kernel_spmd(nc, in_maps, core_ids=list(range(B)))
    return np.stack([np.asarray(r["out"]) for r in res.results], axis=0).astype(np.float32)
```
